# Optimizing a Trainium2 kernel written in Bass

```python
import jax, jax.numpy as jnp
from jax import lax
import numpy as np

D_MODEL = 1024
BATCH = 8
SEQ = 2048
DEPTH = 4
DEC_BATCH = 128
DEC_SEQ = 4
PAST_LEN = 16384
PAGE_SIZE = 128

N_MIXERS = 4
N_POOL_L = (DEPTH + 3) // 4
N_CONV_L = (DEPTH + 2) // 4
N_GM_L = (DEPTH + 1) // 4
N_SC_L = DEPTH // 4
POOL_WINDOWS = (2, 4, 8, 16)
N_POOL_GROUPS = len(POOL_WINDOWS)
POOL_GROUP = D_MODEL // N_POOL_GROUPS
POOL_BUF = max(POOL_WINDOWS) - 1
CONV_WIDTH = 31
D_CONV = D_MODEL
CONV_BUF = CONV_WIDTH - 1
CHUNK = 128
D_GM = D_MODEL
N_GM_GROUPS = 4
GM_GROUP = D_GM // N_GM_GROUPS
SC_WIDTH = 3
SC_BUF = SC_WIDTH - 1
D_FF = -(-8 * D_MODEL // (3 * 256)) * 256
EPS = 1e-6

kernel_name = 'hybrid_pool_conv_gmlp_shortconv_decode_step'


def _rmsnorm(x, g):
    xf = x.astype(jnp.float32)
    y = xf * lax.rsqrt(jnp.mean(xf * xf, axis=-1, keepdims=True) + EPS)
    return (y * g.astype(jnp.float32)).astype(x.dtype)


def _layernorm(x, g, b):
    xf = x.astype(jnp.float32)
    mu = jnp.mean(xf, axis=-1, keepdims=True)
    xc = xf - mu
    y = xc * lax.rsqrt(jnp.mean(xc * xc, axis=-1, keepdims=True) + EPS)
    return (y * g.astype(jnp.float32) + b.astype(jnp.float32)).astype(x.dtype)


def _depthwise_causal(ext, w):
    return lax.conv_general_dilated(ext, w[:, None, :].astype(ext.dtype), window_strides=(1,), padding='VALID', dimension_numbers=('NWC', 'WIO', 'NWC'), feature_group_count=ext.shape[-1])


def _pool_mixer(h, buf, pos0, w, scale):
    B, L, D = h.shape
    ext = jnp.concatenate([buf.astype(h.dtype), h], axis=1)
    cs = jnp.cumsum(ext.astype(jnp.float32), axis=1)
    cs = jnp.concatenate([jnp.zeros_like(cs[:, :1]), cs], axis=1)
    end = cs[:, POOL_BUF + 1:]
    hf = h.astype(jnp.float32)
    pos = pos0 + jnp.arange(L)
    diffs = []
    for g, win in enumerate(POOL_WINDOWS):
        sl = slice(g * POOL_GROUP, (g + 1) * POOL_GROUP)
        start = cs[:, POOL_BUF + 1 - win: POOL_BUF + 1 - win + L, sl]
        cnt = jnp.minimum(win, pos + 1).astype(jnp.float32)[None, :, None]
        diffs.append((end[..., sl] - start) / cnt - hf[..., sl])
    d = jnp.stack(diffs, axis=2).astype(h.dtype)
    y = jnp.einsum('blgc,gce->blge', d, w).reshape(B, L, D)
    return y * scale, ext[:, L:]


def _conformer_conv(h, buf, w_in, b_in, dw, dw_b, ln_g, ln_b, w_out):
    L = h.shape[1]
    z = jnp.einsum('bld,de->ble', h, w_in) + b_in
    a, gate = jnp.split(z, 2, axis=-1)
    g = a * jax.nn.sigmoid(gate)
    ext = jnp.concatenate([buf.astype(g.dtype), g], axis=1)
    c = _depthwise_causal(ext, dw) + dw_b
    c = jax.nn.silu(_layernorm(c, ln_g, ln_b))
    return jnp.einsum('blc,cd->bld', c, w_out), ext[:, L:]


def _chunk_gmlp(h, w_in, ln_g, ln_b, w_s, b_s, w_out):
    B, L, _ = h.shape
    z = jax.nn.gelu(jnp.einsum('bld,de->ble', h, w_in))
    u, v = jnp.split(z, 2, axis=-1)
    v = _layernorm(v, ln_g, ln_b)
    c = min(L, CHUNK)
    n = L // c
    mask = jnp.tril(jnp.ones((c, c), dtype=bool))
    ws = jnp.where(mask[None], w_s[:, :c, :c], 0)
    vg = v.reshape(B, n, c, N_GM_GROUPS, GM_GROUP)
    mixed = jnp.einsum('gts,bnsgc->bntgc', ws, vg) + b_s[:, :c].T[None, None, :, :, None]
    y = u * mixed.reshape(B, L, D_GM)
    return jnp.einsum('ble,ed->bld', y, w_out), v[:, (n - 1) * c:]


def _short_conv(h, buf, w_in, conv_w, w_out):
    L = h.shape[1]
    z = jnp.einsum('bld,de->ble', h, w_in)
    bg, cg, xv = jnp.split(z, 3, axis=-1)
    cx = cg * xv
    ext = jnp.concatenate([buf.astype(cx.dtype), cx], axis=1)
    y = bg * _depthwise_causal(ext, conv_w)
    return jnp.einsum('ble,ed->bld', y, w_out), ext[:, L:]


def _swiglu(h, w_in, w_out):
    gate, up = jnp.split(jnp.einsum('bld,df->blf', h, w_in), 2, axis=-1)
    return jnp.einsum('blf,fd->bld', jax.nn.silu(gate) * up, w_out)


def _trunk(x, pool_buf, conv_buf, sc_buf, pos0, norm_mix, norm_ffn, norm_final, pool_w, pool_scale, conv_w_in, conv_b_in, conv_dw, conv_dw_b, conv_ln_g, conv_ln_b, conv_w_out, gm_w_in, gm_ln_g, gm_ln_b, gm_w_s, gm_b_s, gm_w_out, sc_w_in, sc_conv, sc_w_out, ffn_w_in, ffn_w_out):
    new_pool, new_conv, new_v, new_sc = [], [], [], []
    for i in range(DEPTH):
        m, j = i % N_MIXERS, i // N_MIXERS
        h = _rmsnorm(x, norm_mix[i])
        if m == 0:
            y, s = _pool_mixer(h, pool_buf[j], pos0, pool_w[j], pool_scale[j])
            new_pool.append(s)
        elif m == 1:
            y, s = _conformer_conv(h, conv_buf[j], conv_w_in[j], conv_b_in[j], conv_dw[j], conv_dw_b[j], conv_ln_g[j], conv_ln_b[j], conv_w_out[j])
            new_conv.append(s)
        elif m == 2:
            y, s = _chunk_gmlp(h, gm_w_in[j], gm_ln_g[j], gm_ln_b[j], gm_w_s[j], gm_b_s[j], gm_w_out[j])
            new_v.append(s)
        else:
            y, s = _short_conv(h, sc_buf[j], sc_w_in[j], sc_conv[j], sc_w_out[j])
            new_sc.append(s)
        x = x + y
        x = x + _swiglu(_rmsnorm(x, norm_ffn[i]), ffn_w_in[i], ffn_w_out[i])
    return _rmsnorm(x, norm_final), jnp.stack(new_pool), jnp.stack(new_conv), jnp.stack(new_v), jnp.stack(new_sc)


def setup_inputs(seed: int = 0) -> dict:
    key = jax.random.key(seed)
    ks = iter(jax.random.split(key, 40))
    f32 = jnp.float32

    def nrm(shape, scale):
        return jax.random.normal(next(ks), shape, f32) * scale

    def gain(shape):
        return 1.0 + nrm(shape, 0.1)

    D = D_MODEL
    return {
        'x_prompt': nrm((BATCH, SEQ, D), 1.0),
        'x_sample': nrm((DEC_BATCH, DEC_SEQ, D), 1.0),
        'state_pool': nrm((N_POOL_L, DEC_BATCH, POOL_BUF, D), 1.0),
        'state_conv': nrm((N_CONV_L, DEC_BATCH, CONV_BUF, D_CONV), 0.5),
        'state_shortconv': nrm((N_SC_L, DEC_BATCH, SC_BUF, D), 1.0),
        'norm_mix': gain((DEPTH, D)),
        'norm_ffn': gain((DEPTH, D)),
        'norm_final': gain((D,)),
        'pool_w': nrm((N_POOL_L, N_POOL_GROUPS, POOL_GROUP, POOL_GROUP), POOL_GROUP ** -0.5),
        'pool_scale': gain((N_POOL_L, D)),
        'conv_w_in': nrm((N_CONV_L, D, 2 * D_CONV), D ** -0.5),
        'conv_b_in': nrm((N_CONV_L, 2 * D_CONV), 0.02),
        'conv_dw': nrm((N_CONV_L, CONV_WIDTH, D_CONV), CONV_WIDTH ** -0.5),
        'conv_dw_b': nrm((N_CONV_L, D_CONV), 0.02),
        'conv_ln_g': gain((N_CONV_L, D_CONV)),
        'conv_ln_b': nrm((N_CONV_L, D_CONV), 0.02),
        'conv_w_out': nrm((N_CONV_L, D_CONV, D), D_CONV ** -0.5),
        'gm_w_in': nrm((N_GM_L, D, 2 * D_GM), D ** -0.5),
        'gm_ln_g': gain((N_GM_L, D_GM)),
        'gm_ln_b': nrm((N_GM_L, D_GM), 0.02),
        'gm_w_s': nrm((N_GM_L, N_GM_GROUPS, CHUNK, CHUNK), CHUNK ** -0.5),
        'gm_b_s': gain((N_GM_L, N_GM_GROUPS, CHUNK)),
        'gm_w_out': nrm((N_GM_L, D_GM, D), D_GM ** -0.5),
        'sc_w_in': nrm((N_SC_L, D, 3 * D), D ** -0.5),
        'sc_conv': nrm((N_SC_L, SC_WIDTH, D), SC_WIDTH ** -0.5),
        'sc_w_out': nrm((N_SC_L, D, D), D ** -0.5),
        'ffn_w_in': nrm((DEPTH, D, 2 * D_FF), D ** -0.5),
        'ffn_w_out': nrm((DEPTH, D_FF, D), D_FF ** -0.5),
    }


def reference(x_prompt, x_sample, state_pool, state_conv, state_shortconv, norm_mix, norm_ffn, norm_final, pool_w, pool_scale, conv_w_in, conv_b_in, conv_dw, conv_dw_b, conv_ln_g, conv_ln_b, conv_w_out, gm_w_in, gm_ln_g, gm_ln_b, gm_w_s, gm_b_s, gm_w_out, sc_w_in, sc_conv, sc_w_out, ffn_w_in, ffn_w_out):
    weights = (norm_mix, norm_ffn, norm_final, pool_w, pool_scale, conv_w_in, conv_b_in, conv_dw, conv_dw_b, conv_ln_g, conv_ln_b, conv_w_out, gm_w_in, gm_ln_g, gm_ln_b, gm_w_s, gm_b_s, gm_w_out, sc_w_in, sc_conv, sc_w_out, ffn_w_in, ffn_w_out)
    B = x_prompt.shape[0]
    dt = x_prompt.dtype
    pool0 = jnp.zeros((N_POOL_L, B, POOL_BUF, D_MODEL), dt)
    conv0 = jnp.zeros((N_CONV_L, B, CONV_BUF, D_CONV), dt)
    sc0 = jnp.zeros((N_SC_L, B, SC_BUF, D_MODEL), dt)
    y_prompt, pool_p, conv_p, v_p, sc_p = _trunk(x_prompt, pool0, conv0, sc0, 0, *weights)
    y_sample, pool_s, conv_s, v_s, sc_s = _trunk(x_sample, state_pool, state_conv, state_shortconv, PAST_LEN, *weights)
    return (y_prompt, y_sample, pool_p, pool_s, conv_p, conv_s, v_p, v_s, sc_p, sc_s)
```

```python
from contextlib import ExitStack
import numpy as np
import concourse.bass as bass
import concourse.mybir as mybir
from concourse.bass_utils import run_bass_kernel_spmd
import os as _os


def _dev(key, default=None):
    return _os.environ.get(key, default) if _os.environ.get("MK_DEV") == "1" else default

F32 = mybir.dt.float32
BF16 = mybir.dt.bfloat16
ALU = mybir.AluOpType
AF = mybir.ActivationFunctionType

D = 1024
SEQ = 2048
NSEQ = 16
DEC = 4
NTOK = SEQ + NSEQ * DEC
DFF = 2816
TILES = [(0, 512), (512, 1024), (1024, 1536), (1536, 2048), (2048, 2112)]
FTILES = [(0, 448), (448, 896), (896, 1344), (1344, 1792), (1792, 2112)]
EPS = 1e-6


class Sched:
    ENGS = ("pe", "act", "dve", "pool", "sp")
    NS = 12

    def __init__(self):
        self.ops = {e: [] for e in self.ENGS}
        self.res = {}
        self.ndma = {e: 0 for e in self.ENGS}
        import os
        self.same_skip = int(_dev('KSAMESKIP', '1000000000'))

    def op(self, eng, fn, reads=(), writes=(), dma=False):
        deps = set()
        idx = len(self.ops[eng])
        if dma:
            ref = ("d", eng, self.ndma[eng])
            self.ndma[eng] += 1
        else:
            ref = ("c", eng, idx)
        for r in reads:
            st = self.res.get(r)
            if st is not None and st[0] is not None:
                deps.add(st[0])
        for r in writes:
            st = self.res.get(r)
            if st is not None:
                if st[0] is not None:
                    deps.add(st[0])
                deps.update(st[1])
        for r in reads:
            st = self.res.setdefault(r, [None, []])
            st[1].append(ref)
        for r in writes:
            self.res[r] = [ref, []]
        deps.discard(ref)
        keep = set()
        for d in deps:
            if d[0] == "c" and d[1] == eng and not dma:
                if eng == "pe":
                    continue
                if idx - d[2] > self.same_skip:
                    continue
            keep.add(d)
        self.ops[eng].append(dict(fn=fn, deps=keep, dma=dma, ref=ref, signal=False, count=None, tag=(tuple(reads), tuple(writes))))
        return ref

    def barrier(self):
        last = {}
        for e in ("pe", "act", "dve", "pool"):
            for i in range(len(self.ops[e]) - 1, -1, -1):
                o = self.ops[e][i]
                if not o["dma"] and o["fn"] is not None:
                    last[e] = ("c", e, i)
                    break
        dmadeps = set()
        for q in self.ENGS:
            n = self.ndma[q]
            for i in range(max(0, n - self.NS), n):
                dmadeps.add(("d", q, i))
        for e in self.ENGS:
            deps = set(v for k, v in last.items() if k != e) | dmadeps
            self.ops[e].append(dict(fn=None, deps=deps, dma=False, ref=("c", e, len(self.ops[e])), signal=False, count=None))
        self.res = {}

    def finalize(self):
        for e in self.ENGS:
            for o in self.ops[e]:
                for d in o["deps"]:
                    if d[0] == "c":
                        self.ops[d[1]][d[2]]["signal"] = True
        for e in self.ENGS:
            c = 0
            for o in self.ops[e]:
                if o["signal"] and not o["dma"]:
                    c += 1
                    o["count"] = c

    def emit(self, nc, stack):
        self.finalize()
        csem = {e: stack.enter_context(nc.semaphore("s_" + e)) for e in ("pe", "act", "dve", "pool")}
        dsem = {}
        for q in self.ENGS:
            if self.ndma[q] > 0:
                dsem[q] = [stack.enter_context(nc.semaphore("d_%s%d" % (q, i))) for i in range(min(self.NS, self.ndma[q]))]
        block = stack.enter_context(nc.Block())
        import os
        DUMP = open(_dev("KDUMP"), "w") if _dev("KDUMP") else None
        semname = {id(v): "s_" + k for k, v in csem.items()}
        for q in dsem:
            for i, v in enumerate(dsem[q]):
                semname[id(v)] = "d_%s%d" % (q, i)
        NS = self.NS
        sched = self

        def run(eng_name, eng):
            waited = {}

            def wait(sem, val):
                k = id(sem)
                if waited.get(k, 0) >= val:
                    return
                waited[k] = val
                if DUMP is not None:
                    DUMP.write("%s   wait %s >= %s\n" % (eng_name, semname[k], val))
                eng.wait_ge(sem, val)

            for o in sched.ops[eng_name]:
                for d in sorted(o["deps"]):
                    if d[0] == "c":
                        wait(csem[d[1]], sched.ops[d[1]][d[2]]["count"])
                    else:
                        q, i = d[1], d[2]
                        wait(dsem[q][i % NS], 16 * (i // NS + 1))
                if o["fn"] is None:
                    continue
                if o["dma"]:
                    i = o["ref"][2]
                    if i >= NS:
                        wait(dsem[eng_name][i % NS], 16 * (i // NS))
                inst = o["fn"](eng)
                if DUMP is not None:
                    DUMP.write("%s op %s %s sig=%s cnt=%s\n" % (eng_name, o["ref"], o.get("tag"), o["signal"], o["count"]))
                if o["dma"]:
                    i = o["ref"][2]
                    inst.then_inc(dsem[eng_name][i % NS], 16)
                elif o["signal"]:
                    inst.then_inc(csem[eng_name], 1)
            if eng_name == "sp":
                for q in dsem:
                    n = sched.ndma[q]
                    for s in range(len(dsem[q])):
                        cnt = (n - s + NS - 1) // NS
                        if cnt > 0:
                            wait(dsem[q][s], 16 * cnt)

        @block.tensor
        def _(e):
            run("pe", e)

        @block.scalar
        def _(e):
            run("act", e)

        @block.vector
        def _(e):
            run("dve", e)

        @block.gpsimd
        def _(e):
            run("pool", e)

        @block.sync
        def _(e):
            run("sp", e)


NW = 53100
O_X = 0
O_W = O_X + 8 * NTOK
WU = 2048
O_CV = O_W + 6 * WU
O_ID = O_CV + 512
O_ONES = O_ID + 128
O_EPS = O_ONES + 64
O_INVC = O_EPS + 4
O_H = O_INVC + 64
HW_ = 9472
O_SCR = O_H + HW_
SCRW = NW - O_SCR

IN_NAMES = ["xp", "xs", "spool", "sconv", "ssc", "norm_mix", "norm_ffn", "norm_final", "pool_w", "pool_scale",
            "conv_w_in", "conv_b_in", "conv_dw", "conv_dw_b", "conv_ln_g", "conv_ln_b", "conv_w_out",
            "gm_w_in", "gm_ln_g", "gm_ln_b", "gm_w_s", "gm_b_s", "gm_w_out",
            "sc_w_in", "sc_conv", "sc_w_out", "ffn_w_in", "ffn_w_out"]
IN_SHAPES = {
    "xp": [SEQ, D], "xs": [64, D], "spool": [240, D], "sconv": [480, D], "ssc": [32, D],
    "norm_mix": [4, D], "norm_ffn": [4, D], "norm_final": [1, D], "pool_w": [4, 256, 256], "pool_scale": [1, D],
    "conv_w_in": [D, 2 * D], "conv_b_in": [1, 2 * D], "conv_dw": [31, D], "conv_dw_b": [1, D],
    "conv_ln_g": [1, D], "conv_ln_b": [1, D], "conv_w_out": [D, D],
    "gm_w_in": [D, 2 * D], "gm_ln_g": [1, D], "gm_ln_b": [1, D], "gm_w_s": [4, 128, 128], "gm_b_s": [4, 128],
    "gm_w_out": [D, D], "sc_w_in": [D, 3 * D], "sc_conv": [3, D], "sc_w_out": [D, D],
    "ffn_w_in": [4, D, 2 * DFF], "ffn_w_out": [4, DFF, D],
}
OUT_SHAPES = {
    "yp": [SEQ, D], "ys": [64, D], "pool_p": [15, D], "pool_s": [240, D], "conv_p": [30, D], "conv_s": [480, D],
    "v_p": [128, D], "v_s": [64, D], "sc_p": [2, D], "sc_s": [32, D],
}


class Kern:
    def __init__(self, nc, S, A, PS, dr):
        self.nc, self.S, self.A, self.PS, self.dr = nc, S, A, PS, dr
        self.nbank = 0
        self.X = self.f32(O_X, 8 * NTOK).rearrange("p (c t) -> p c t", c=8)
        self.CV = self.f32(O_CV, 512)
        self.ident = self.f32(O_ID, 128)
        self.ones = self.b16(O_ONES, 64)
        self.eps = self.f32(O_EPS, 4)
        self.invc = self.f32(O_INVC, 64).rearrange("p (g t) -> p g t", g=4)
        self.cvcol = {}
        self.scr_off = 0
        self.stg_i = 0
        self.pref = set()
        self.reserved = set()

    def f32(self, off, n):
        return self.A[:, off:off + n]

    def b16(self, off, n):
        return self.A[:, off:off + n].bitcast(BF16)

    def scr_reset(self):
        self.scr_off = 0

    def scr(self, n, bf=False):
        off = O_SCR + self.scr_off
        self.scr_off += n
        assert self.scr_off <= SCRW, (self.scr_off, SCRW)
        return self.b16(off, n) if bf else self.f32(off, n)

    def hreg(self, off, n, bf=False):
        assert off + n <= HW_
        return self.b16(O_H + off, n) if bf else self.f32(O_H + off, n)

    def wunit(self, u, nu=1):
        return self.b16(O_W + u * WU, nu * WU)

    def bank(self):
        while True:
            b = self.nbank % 8
            self.nbank += 1
            if b not in self.reserved:
                return b

    def mm(self, out, lhsT, rhs, start, stop, r, w):
        self.S.op("pe", lambda e: e.matmul(out, lhsT=lhsT, rhs=rhs, start=start, stop=stop), r, w)

    def tr(self, out, in_, ident, r, w):
        self.S.op("pe", lambda e: e.transpose(out=out, in_=in_, identity=ident), r, w)

    def act(self, out, in_, func, r, w, bias=None, scale=None, accum_out=None):
        kw = {}
        if bias is not None:
            kw["bias"] = bias
        if scale is not None:
            kw["scale"] = scale
        if accum_out is not None:
            kw["accum_out"] = accum_out
        self.S.op("act", lambda e: e.activation(out=out, in_=in_, func=func, **kw), r, w)

    def tt(self, out, in0, in1, op, r, w, eng="dve"):
        self.S.op(eng, lambda e: e.tensor_tensor(out=out, in0=in0, in1=in1, op=op), r, w)

    def ts(self, out, in0, s1, s2, op0, op1, r, w, eng="dve"):
        if op1 is None:
            self.S.op(eng, lambda e: e.tensor_scalar(out=out, in0=in0, scalar1=s1, scalar2=None, op0=op0), r, w)
        else:
            self.S.op(eng, lambda e: e.tensor_scalar(out=out, in0=in0, scalar1=s1, scalar2=s2, op0=op0, op1=op1), r, w)

    def stt(self, out, in0, scalar, in1, op0, op1, r, w, accum_out=None):
        if accum_out is None:
            self.S.op("dve", lambda e: e.scalar_tensor_tensor(out=out, in0=in0, scalar=scalar, in1=in1, op0=op0, op1=op1), r, w)
        else:
            self.S.op("dve", lambda e: e.scalar_tensor_tensor(out=out, in0=in0, scalar=scalar, in1=in1, op0=op0, op1=op1, accum_out=accum_out), r, w)

    def cp(self, eng, out, in_, r, w):
        if eng == "act":
            self.S.op("act", lambda e: e.copy(out=out, in_=in_), r, w)
        else:
            self.S.op(eng, lambda e: e.tensor_copy(out=out, in_=in_), r, w)

    def recip(self, out, in_, r, w):
        self.S.op("dve", lambda e: e.reciprocal(out=out, in_=in_), r, w)

    def memset(self, eng, ap, val, w):
        self.S.op(eng, lambda e: e.memset(ap, val), [], w)

    def dma(self, q, out, in_, r, w, **kw):
        self.S.op(q, lambda e: e.dma_start(out=out, in_=in_, **kw), r, w, dma=True)

    def load_T(self, rows_ap, R, dst_fn, dst_res, stg, stg_res, eng_sel=0):
        self.dma("sp", stg[:R, :], rows_ap, [], [stg_res])
        for half in range(2):
            b = self.bank()
            for cc in range(4):
                c = half * 4 + cc
                self.tr(self.PS[:, b, cc * 128:cc * 128 + R], stg[:R, c * 128:(c + 1) * 128], self.ident[:R, :R],
                        [stg_res, "ident"], [("ps", b)])
            for cc in range(4):
                c = half * 4 + cc
                dst_fn(c, self.PS[:, b, cc * 128:cc * 128 + R], ("ps", b))

    def store_T(self, src_fn, src_res, R, rows_dma_fn, stg, stg_res):
        for half in range(2):
            b = self.bank()
            for cc in range(4):
                c = half * 4 + cc
                self.tr(self.PS[:R, b, cc * 128:(cc + 1) * 128], src_fn(c), self.ident[:, :], list(src_res) + ["ident"], [("ps", b)])
            eng = "act" if half == 0 else "dve"
            self.cp(eng, stg[:R, half * 512:(half + 1) * 512], self.PS[:R, b, :], [("ps", b)], [stg_res])
        rows_dma_fn(stg, stg_res)

    def setup_consts(self):
        S = self.S
        ident = self.ident
        self.memset("pool", ident[:, :], 0.0, ["ident"])
        S.op("pool", lambda e: e.affine_select(out=ident[:, :], in_=ident[:, :], pattern=[[-1, 128]], compare_op=ALU.not_equal,
                                               fill=1.0, base=0, channel_multiplier=1), ["ident"], ["ident"])
        self.memset("pool", self.ones[:, :], 1.0 / 1024.0, ["ones"])
        self.memset("pool", self.eps[:, :], EPS, ["eps"])
        for g, win in enumerate((2, 4, 8, 16)):
            self.memset("pool", self.invc[:, g, :], 1.0 / win, ["invc"])
            for t in range(win - 1):
                self.memset("pool", self.invc[:, g, t:t + 1], 1.0 / (t + 1), ["invc"])
        dr = self.dr
        vecs = [("norm_mix", dr["norm_mix"].rearrange("l (c p) -> (l c) p", p=128), 32),
                ("norm_ffn", dr["norm_ffn"].rearrange("l (c p) -> (l c) p", p=128), 32),
                ("norm_final", dr["norm_final"].rearrange("l (c p) -> (l c) p", p=128), 8),
                ("pool_scale", dr["pool_scale"].rearrange("l (c p) -> (l c) p", p=128), 8),
                ("conv_b_in", dr["conv_b_in"].rearrange("l (c p) -> (l c) p", p=128), 16),
                ("conv_dw", dr["conv_dw"].rearrange("l (c p) -> (l c) p", p=128), 248),
                ("conv_dw_b", dr["conv_dw_b"].rearrange("l (c p) -> (l c) p", p=128), 8),
                ("conv_ln_g", dr["conv_ln_g"].rearrange("l (c p) -> (l c) p", p=128), 8),
                ("conv_ln_b", dr["conv_ln_b"].rearrange("l (c p) -> (l c) p", p=128), 8),
                ("sc_conv", dr["sc_conv"].rearrange("l (c p) -> (l c) p", p=128), 24),
                ("gm_ln_g", dr["gm_ln_g"].rearrange("l (c p) -> (l c) p", p=128), 8),
                ("gm_ln_b", dr["gm_ln_b"].rearrange("l (c p) -> (l c) p", p=128), 8)]
        self.scr_reset()
        stg = self.scr(4 * 128).rearrange("p (j q) -> p j q", j=4)
        row = 0
        stres = {j: [] for j in range(4)}
        for name, ap, n in vecs:
            self.cvcol[name] = row
            done = 0
            while done < n:
                j, r0 = divmod(row + done, 128)
                k = min(n - done, 128 - r0)
                rn = ("cvstg", j, r0)
                stres[j].append(rn)
                self.dma("sp", stg[r0:r0 + k, j, :], ap[done:done + k, :], [], [rn])
                done += k
            row += n
        assert row <= 512
        ntile = (row + 127) // 128
        b = self.bank()
        for j in range(ntile):
            R = min(128, row - j * 128)
            self.tr(self.PS[:, b, j * 128:j * 128 + R], stg[:R, j, :], self.ident[:R, :R], stres[j] + ["ident"], [("ps", b)])
        self.cp("dve", self.CV[:, :row], self.PS[:, b, :row], [("ps", b)], ["cv"])

    def cv(self, name, i):
        c = self.cvcol[name] + i
        return self.CV[:, c:c + 1]

    def norm_a(self, ti, sq, sd, rstd, tiles=None):
        t0, t1 = (tiles or TILES)[ti]
        T = t1 - t0
        X = self.X
        b = self.bank()
        for c in range(8):
            self.act(sq[:, c % 4, :T], X[:, c, t0:t1], AF.Square, [("x", c, ti)], [("sq", c % 4)])
            self.mm(self.PS[:, b, :T], self.ones[:, :], sq[:, c % 4, :T], c == 0, c == 7, [("sq", c % 4), "ones"], [("ps", b)])
        self.act(sd[:, :T], self.PS[:, b, :T], AF.Sqrt, [("ps", b), "eps"], ["sd"], bias=self.eps[:, 0:1])
        self.recip(rstd[:, :T], sd[:, :T], ["sd"], ["rstd"])

    def norm_b(self, ti, gname, gidx0, out_fn, rstd, extra_fn=None, tiles=None):
        t0, t1 = (tiles or TILES)[ti]
        T = t1 - t0
        X = self.X
        for c in range(8):
            out, wres, view = out_fn(c)
            xin = X[:, c, t0:t1]
            rs = rstd[:, :T]
            if view is not None:
                xin, rs = view(xin), view(rs)
            self.stt(out, xin, self.cv(gname, gidx0 + c), rs, ALU.mult, ALU.mult, [("x", c, ti), "rstd", "cv"], [wres])
            if extra_fn is not None:
                extra_fn(c, rstd)

    def norm(self, ti, gname, gidx0, out_fn, sq, sd, rstd, extra_fn=None, tiles=None):
        self.norm_a(ti, sq, sd, rstd, tiles)
        self.norm_b(ti, gname, gidx0, out_fn, rstd, extra_fn, tiles)

    def norm_scratch(self):
        sq = self.scr(1024, bf=True).rearrange("p (c t) -> p c t", c=4)
        sd = self.scr(512)
        rstd = self.scr(512)
        return sq, sd, rstd

    def load_wblock(self, dst, src2d, col0, ncol, res, rows=D, key=None):
        if key is not None:
            if key in self.pref:
                return
            self.pref.add(key)
        src = src2d.rearrange("(k p) f -> p k f", p=128)[:, :, col0:col0 + ncol]
        self.dma("pool", dst, src, [], [res])

    def mixer_win(self, name, j):
        dst = self.wunit(j).rearrange("p (k f) -> p k f", k=8)
        self.load_wblock(dst, self.dr[name], j * 512, 512, ("W", j), key=(name, j))
        return dst

    def ffn_load(self, L, bi, jbase):
        blocks = [(0, 4), (4, 4), (8, 4), (12, 4), (16, 4), (20, 2)]
        f0, nf = blocks[bi]
        p = (jbase + bi) % 2
        w_in = self.dr["ffn_w_in"][L]
        w_out = self.dr["ffn_w_out"][L]
        wg = self.wunit(3 * p)[:, :8 * nf * 128].rearrange("p (k f) -> p k f", k=8)
        wu = self.wunit(3 * p + 1)[:, :8 * nf * 128].rearrange("p (k f) -> p k f", k=8)
        wo = self.wunit(3 * p + 2)[:, :nf * 1024].rearrange("p (j d) -> p j d", j=nf)
        if ("ffn", L, bi) not in self.pref:
            self.pref.add(("ffn", L, bi))
            self.load_wblock(wg, w_in, f0 * 128, nf * 128, ("W", 3 * p))
            self.load_wblock(wu, w_in, DFF + f0 * 128, nf * 128, ("W", 3 * p + 1))
            src = w_out[f0 * 128:(f0 + nf) * 128, :].rearrange("(j p) d -> p j d", p=128)
            self.dma("pool", wo, src, [], [("W", 3 * p + 2)])
        return (wg, wu, wo, p)

    def phase_input(self):
        self.scr_reset()
        stg = [self.scr(1024) for _ in range(4)]
        X = self.X
        n = 0
        import os
        for j in range(int(_dev("KNIN", "17"))):
            if j < 16:
                rows, R, col0 = self.dr["xp"][j * 128:(j + 1) * 128, :], 128, j * 128
            else:
                rows, R, col0 = self.dr["xs"], 64, SEQ
            ti = min(col0 // 512, 4)
            s = stg[j % 4]
            sres = ("stg", j % 4)

            def dst(c, ps, psres, col0=col0, R=R, ti=ti, j=j):
                nonlocal n
                n += 1
                eng = "act" if (c // 4) % 2 == 0 else "dve"
                self.cp(eng, X[:, c, col0:col0 + R], ps, [psres], [("x", c, ti)])
            self.load_T(rows, R, dst, None, s, sres)

    def phase_ffn(self, L, jbase, next_prefetch=None, pre_hook=None):
        self.scr_reset()
        sq, sd, rstd = self.norm_scratch()
        actb = [self.scr(1024, bf=True).rearrange("p (f t) -> p f t", f=4) for _ in range(2)]
        sg = [self.scr(512) for _ in range(2)]
        H = self.hreg(0, 8 * NTOK // 2, bf=True).rearrange("p (c t) -> p c t", c=8)
        X, PS = self.X, self.PS
        def do_norm(ti):
            if ti >= 5:
                return
            t0, t1 = FTILES[ti]
            self.norm(ti, "norm_ffn", L * 8, lambda c, t0=t0, t1=t1, ti=ti: (H[:, c, t0:t1], ("H", c, ti), None), sq, sd, rstd, tiles=FTILES)
        blocks = [(0, 4), (4, 4), (8, 4), (12, 4), (16, 4), (20, 2)]
        w_in = self.dr["ffn_w_in"][L]
        w_out = self.dr["ffn_w_out"][L]
        seq = []
        for bi, (f0, nf) in enumerate(blocks):
            for ti in range(5):
                seq.append((bi, ti))
        loaded = set()
        wv = {}

        def ensure_loaded(bi):
            if bi in loaded or bi >= len(blocks):
                return
            loaded.add(bi)
            wv[bi] = self.ffn_load(L, bi, jbase)

        def GU(n):
            bi, ti = seq[n]
            f0, nf = blocks[bi]
            wg, wu, wo, p = wv[bi]
            t0, t1 = FTILES[ti]
            T = t1 - t0
            ab = actb[n % 2]
            for fc in range(nf):
                bg, bu = self.bank(), self.bank()
                for k in range(8):
                    self.mm(PS[:, bg, :T], wg[:, k, fc * 128:(fc + 1) * 128], H[:, k, t0:t1], k == 0, k == 7,
                            [("W", 3 * p), ("H", k, ti)], [("ps", bg)])
                for k in range(8):
                    self.mm(PS[:, bu, :T], wu[:, k, fc * 128:(fc + 1) * 128], H[:, k, t0:t1], k == 0, k == 7,
                            [("W", 3 * p + 1), ("H", k, ti)], [("ps", bu)])
                s = sg[fc % 2]
                self.act(s[:, :T], PS[:, bg, :T], AF.Silu, [("ps", bg)], [("sg", fc % 2)])
                self.tt(ab[:, fc, :T], s[:, :T], PS[:, bu, :T], ALU.mult, [("sg", fc % 2), ("ps", bu)], [("actb", n % 2, fc)])

        def Y(n):
            bi, ti = seq[n]
            f0, nf = blocks[bi]
            wg, wu, wo, p = wv[bi]
            t0, t1 = FTILES[ti]
            T = t1 - t0
            ab = actb[n % 2]
            for m in range(8):
                b = self.bank()
                for fc in range(nf):
                    self.mm(PS[:, b, :T], wo[:, fc, m * 128:(m + 1) * 128], ab[:, fc, :T], fc == 0, fc == nf - 1,
                            [("W", 3 * p + 2), ("actb", n % 2, fc)], [("ps", b)])
                self.tt(X[:, m, t0:t1], X[:, m, t0:t1], PS[:, b, :T], ALU.add, [("x", m, ti), ("ps", b)], [("x", m, ti)])

        ensure_loaded(0)
        if pre_hook is not None:
            pre_hook()
        do_norm(0)
        for n in range(len(seq)):
            if seq[n][0] == 0:
                do_norm(seq[n][1] + 1)
            GU(n)
            if n > 0:
                Y(n - 1)
            if seq[n][1] == 0:
                ensure_loaded(seq[n][0] + 1)
                if seq[n][0] == len(blocks) - 1 and next_prefetch is not None:
                    next_prefetch()
        Y(len(seq) - 1)
        return jbase + len(blocks)

    def phase_final(self):
        self.scr_reset()
        sq, sd, rstd = self.norm_scratch()
        yts = [self.scr(4096).rearrange("p (c t) -> p c t", c=8) for _ in range(2)]
        stg = [self.scr(1024) for _ in range(3)]
        n = 0

        def fnorm(ti):
            if ti < 5:
                T = TILES[ti][1] - TILES[ti][0]
                yt = yts[ti % 2]
                self.norm(ti, "norm_final", 0, lambda c, T=T, yt=yt, ti=ti: (yt[:, c, :T], ("yt", ti % 2, c), None), sq, sd, rstd)
        fnorm(0)
        for ti in range(5):
            t0, t1 = TILES[ti]
            T = t1 - t0
            yt = yts[ti % 2]
            fnorm(ti + 1)
            for q in range((T + 127) // 128):
                R = min(128, T - q * 128)
                s, sres = stg[n % 3], ("ostg", n % 3)
                n += 1
                if ti < 4:
                    rows = self.dr["yp"][t0 + q * 128:t0 + q * 128 + R, :]
                else:
                    rows = self.dr["ys"]
                self.store_T(lambda c, q=q, R=R, yt=yt: yt[:, c, q * 128:q * 128 + R], [("yt", ti % 2, c) for c in range(8)], R,
                             lambda s_, r_, rows=rows, R=R: self.dma("sp", rows, s_[:R, :], [r_], []), s, sres)

    def phase_pool(self):
        self.scr_reset()
        sq, sd, rstd = self.norm_scratch()
        wsA = self.scr(528)
        wsB = self.scr(528)
        dt_ = self.scr(2048, bf=True).rearrange("p (c t) -> p c t", c=8)
        HF = self.scr(640).rearrange("p (c t) -> p c t", c=8)
        stg = self.scr(1024)
        tmp15 = [self.scr(16), self.scr(16)]
        identb = self.scr(64, bf=True)
        Iw = self.scr(512, bf=True).rearrange("p (j m) -> p j m", j=8)
        wic = self.scr(64).rearrange("p (g t) -> p g t", g=4)
        X, PS, dr = self.X, self.PS, self.dr
        self.cp("act", identb[:, :], self.ident[:, :], ["ident"], ["identb"])
        for g in range(4):
            w_ = 2.0 ** (g + 1)
            self.ts(Iw[:, 2 * g, :], identb[:, :], 1.0 / w_ - 1.0, None, ALU.mult, None, ["identb"], ["Iw"])
            self.ts(Iw[:, 2 * g + 1, :], identb[:, :], 1.0 / w_, None, ALU.mult, None, ["identb"], ["Iw"])
            self.ts(wic[:, g, :], self.invc[:, g, :], w_, None, ALU.mult, None, ["invc"], ["wic"])
        HPW = 2368
        HP = self.hreg(0, 8 * HPW // 2, bf=True).rearrange("p (c t) -> p c t", c=8)
        SB = 2064
        pw = self.wunit(5)[:, :2048].rearrange("p (g k e) -> p g k e", g=4, k=2)
        self.dma("pool", pw, dr["pool_w"].rearrange("g (k p) e -> p g k e", p=128), [], [("W", 5)])
        self.ffn_load(0, 0, 0)
        self.memset("pool", HP[:, :, 0:16], 0.0, ["hp_pad"])
        for half in range(2):
            def dst(c, ps, psres, half=half):
                o = HP[:, c, SB + half * 152:SB + half * 152 + 152].rearrange("p (b r) -> p b r", r=19)[:, :, 0:15]
                self.cp("act", o, ps.rearrange("p (b r) -> p b r", r=15), [psres], [("hp_s", c)])
            self.load_T(dr["spool"][half * 120:(half + 1) * 120, :], 120, dst, None, stg, "stg")
        self.dma("sp", dr["pool_s"].rearrange("(b r) d -> b r d", r=15)[:, 0:11, :],
                 dr["spool"].rearrange("(b r) d -> b r d", r=15)[:, 4:15, :], [], [])
        sview = lambda ap: ap.rearrange("p (b t) -> p b t", t=4)
        norm_jobs = []
        for ti in range(5):
            t0, t1 = TILES[ti]
            T = t1 - t0
            if ti < 4:
                ofn = lambda c, t0=t0, t1=t1, ti=ti: (HP[:, c, 16 + t0:16 + t1], ("hp", c, ti), None)
            else:
                ofn = lambda c: (HP[:, c, SB:SB + 304].rearrange("p (b r) -> p b r", r=19)[:, :, 15:19], ("hp_s", c), sview)
            extra = None
            if ti == 3:
                def extra(c, rstd_):
                    self.stt(HF[:, c, 0:15], X[:, c, 2033:2048], self.cv("norm_mix", c), rstd_[:, 497:512], ALU.mult, ALU.mult,
                             [("x", c, 3), "rstd", "cv"], [("hf", c)])
            if ti == 4:
                def extra(c, rstd_):
                    self.stt(HF[:, c, 16:80], X[:, c, 2048:2112], self.cv("norm_mix", c), rstd_[:, 0:64], ALU.mult, ALU.mult,
                             [("x", c, 4), "rstd", "cv"], [("hfs", c)])
            norm_jobs.append((ti, ofn, extra))
        def do_pool_norm(ti):
            if ti < 5:
                ti_, ofn, extra = norm_jobs[ti]
                self.norm(ti_, "norm_mix", 0, ofn, sq, sd, rstd, extra)
        do_pool_norm(0)
        for ti in range(5):
            t0, t1 = TILES[ti]
            T = t1 - t0
            do_pool_norm(ti + 1)
            for c in range(8):
                g = c // 2
                win = 2 ** (g + 1)
                b = self.bank()
                if ti < 4:
                    hres = [("hp", c, ti)] + ([("hp", c, ti - 1)] if ti > 0 else ["hp_pad"])
                else:
                    hres = [("hp_s", c)]
                for k in range(win):
                    lw = Iw[:, 2 * g + (0 if k == 0 else 1), :]
                    if ti < 4:
                        rhs = HP[:, c, 16 + t0 - k:16 + t0 - k + T]
                        po = PS[:, b, :T]
                    else:
                        rhs = HP[:, c, SB:SB + 304].rearrange("p (b r) -> p b r", r=19)[:, :, 15 - k:19 - k]
                        po = PS[:, b, :T].rearrange("p (b t) -> p b t", t=4)
                    self.mm(po, lw, rhs, k == 0, k == win - 1, ["Iw"] + hres, [("ps", b)])
                self.cp("act", dt_[:, c, :T], PS[:, b, :T], [("ps", b)], [("dt", c)])
                if ti == 0 and win > 1:
                    tmp = tmp15[c % 2]
                    hs = HP[:, c, 16:31]
                    self.cp("act", tmp[:, 0:15], PS[:, b, 0:15], [("ps", b)], [("wsx", c % 2)])
                    self.tt(tmp[:, 0:15], tmp[:, 0:15], hs, ALU.add, [("wsx", c % 2)] + hres, [("wsx", c % 2)])
                    self.tt(tmp[:, 0:15], tmp[:, 0:15], wic[:, g, 0:15], ALU.mult, [("wsx", c % 2), "wic"], [("wsx", c % 2)])
                    self.tt(dt_[:, c, 0:15], tmp[:, 0:15], hs, ALU.subtract, [("wsx", c % 2)] + hres, [("dt", c)])
            for g in range(4):
                for m in range(2):
                    b = self.bank()
                    for k in range(2):
                        self.mm(PS[:, b, :T], pw[:, g, k, m * 128:(m + 1) * 128], dt_[:, 2 * g + k, :T], k == 0, k == 1,
                                [("W", 5), ("dt", 2 * g + k)], [("ps", b)])
                    c = 2 * g + m
                    self.stt(X[:, c, t0:t1], PS[:, b, :T], self.cv("pool_scale", c), X[:, c, t0:t1], ALU.mult, ALU.add,
                             [("ps", b), ("x", c, ti), "cv"], [("x", c, ti)])
        self.store_T(lambda c: HF[:, c, 0:15], [("hf", c) for c in range(8)], 15,
                     lambda s_, r_: self.dma("sp", dr["pool_p"], s_[:15, :], [r_], []), stg, "stg")

        def sample_rows(s_, r_):
            for b in range(NSEQ):
                self.dma("sp", dr["pool_s"][b * 15 + 11:b * 15 + 15, :], s_[4 * b:4 * b + 4, :], [r_], [])
        self.store_T(lambda c: HF[:, c, 16:80], [("hfs", c) for c in range(8)], 64, sample_rows, stg, "stg")

    def phase_conv(self):
        self.scr_reset()
        sq, sd, rstd = self.norm_scratch()
        gexts = [self.scr(4 * 544, bf=True).rearrange("p (c t) -> p c t", c=8) for _ in range(2)]
        ct = self.scr(4096).rearrange("p (c t) -> p c t", c=8)
        stg = self.scr(1024)
        Dg = [self.scr(1984, bf=True).rearrange("p (k m) -> p k m", k=31),
              self.hreg(4096, 1984, bf=True).rearrange("p (k m) -> p k m", k=31)]
        identb = self.scr(64, bf=True)
        ht = self.hreg(0, 2048, bf=True).rearrange("p (c t) -> p c t", c=8)
        cn = self.hreg(2048, 2048, bf=True).rearrange("p (c t) -> p c t", c=8)
        cb = [self.hreg(6080 + i * 256, 256, bf=True) for i in range(2)]
        cq = [self.hreg(6592 + i * 256, 256, bf=True) for i in range(2)]
        sig = [self.hreg(7104, 512)]
        mean_sb = self.hreg(7616, 512)
        mv = self.hreg(8128, 512)
        gsm = self.hreg(8640, 512).rearrange("p (c t) -> p c t", c=8)
        gfp = self.hreg(9152, 240).rearrange("p (c t) -> p c t", c=8)
        X, PS, dr = self.X, self.PS, self.dr
        win = [self.mixer_win("conv_w_in", j) for j in range(4)]
        wout = [self.wunit(4 + j).rearrange("p (k f) -> p k f", k=8) for j in range(2)]
        for j in range(2):
            self.load_wblock(wout[j], dr["conv_w_out"], j * 512, 512, ("W", 4 + j))
        self.cp("act", identb[:, :], self.ident[:, :], ["ident"], ["identb"])
        self.dma("sp", dr["conv_s"].rearrange("(b r) d -> b r d", r=30)[:, 0:26, :],
                 dr["sconv"].rearrange("(b r) d -> b r d", r=30)[:, 4:30, :], [], [])
        sview = lambda ap: ap.rearrange("p (b t) -> p b t", t=4)
        dwc = self.cvcol["conv_dw"]
        self.ndg = 0

        def tinfo(ti):
            t0, t1 = TILES[ti]
            return t0, t1, t1 - t0, ti == 4

        def stage_N(ti):
            t0, t1, T, samp = tinfo(ti)
            self.norm(ti, "norm_mix", 8, lambda c, T=T: (ht[:, c, :T], ("ht", c), None), sq, sd, rstd)

        def stage_A(ti):
            t0, t1, T, samp = tinfo(ti)
            gext = gexts[ti % 2]
            gres = ("gext", ti % 2)
            prev, pres = gexts[(ti - 1) % 2], ("gext", (ti - 1) % 2)
            if ti == 0:
                self.memset("pool", gext[:, :, 0:30], 0.0, [gres])
            elif not samp:
                self.cp("pool", gext[:, :, 0:30], prev[:, :, 512:542], [pres], [gres])
            else:
                for q in range(4):
                    def dst(c, ps, psres, q=q):
                        o = gext[:, c, q * 136:(q + 1) * 136].rearrange("p (b r) -> p b r", r=34)[:, :, 0:30]
                        self.cp("act", o, ps.rearrange("p (b r) -> p b r", r=30), [psres], [gres])
                    self.load_T(dr["sconv"][q * 120:(q + 1) * 120, :], 120, dst, None, stg, "stg")
            for m in range(8):
                ba, bg = self.bank(), self.bank()
                for k in range(8):
                    self.mm(PS[:, ba, :T], win[m // 4][:, k, (m % 4) * 128:(m % 4 + 1) * 128], ht[:, k, :T], k == 0, k == 7,
                            [("W", m // 4), ("ht", k)], [("ps", ba)])
                for k in range(8):
                    self.mm(PS[:, bg, :T], win[2 + m // 4][:, k, (m % 4) * 128:(m % 4 + 1) * 128], ht[:, k, :T], k == 0, k == 7,
                            [("W", 2 + m // 4), ("ht", k)], [("ps", bg)])
                s = sig[0]
                self.act(s[:, :T], PS[:, bg, :T], AF.Sigmoid, [("ps", bg), "cv"], ["sig"], bias=self.cv("conv_b_in", 8 + m))
                if not samp:
                    o, pa, sv = gext[:, m, 30:30 + T], PS[:, ba, :T], s[:, :T]
                else:
                    o = gext[:, m, 0:544].rearrange("p (b r) -> p b r", r=34)[:, :, 30:34]
                    pa, sv = sview(PS[:, ba, :T]), sview(s[:, :T])
                self.stt(o, pa, self.cv("conv_b_in", m), sv, ALU.add, ALU.mult, [("ps", ba), "sig", "cv", gres], [gres, ("gx", m)])
                if ti == 3:
                    self.stt(gfp[:, m, :], PS[:, ba, 482:512], self.cv("conv_b_in", m), s[:, 482:512], ALU.add, ALU.mult,
                             [("ps", ba), "sig", "cv"], [("gfp", m)])
                if samp:
                    self.stt(gsm[:, m, :], PS[:, ba, :T], self.cv("conv_b_in", m), s[:, :T], ALU.add, ALU.mult,
                             [("ps", ba), "sig", "cv"], [("gsm", m)])
            if ti == 3:
                self.store_T(lambda c: gfp[:, c, :], [("gfp", c) for c in range(8)], 30,
                             lambda s_, r_: self.dma("sp", dr["conv_p"], s_[:30, :], [r_], []), stg, "stg")
            if samp:
                def sample_rows(s_, r_):
                    for b in range(NSEQ):
                        self.dma("sp", dr["conv_s"][b * 30 + 26:b * 30 + 30, :], s_[4 * b:4 * b + 4, :], [r_], [])
                self.store_T(lambda c: gsm[:, c, :], [("gsm", c) for c in range(8)], 64, sample_rows, stg, "stg")
                self.ffn_load(self.cur_L, 0, self.cur_j)

        def stage_C(ti):
            t0, t1, T, samp = tinfo(ti)
            gext = gexts[ti % 2]
            gres = ("gext", ti % 2)
            b1, b2 = self.bank(), self.bank()
            self.lnb12 = (b1, b2)
            self.reserved |= {b1, b2}

            def stats_mm(c):
                self.mm(PS[:, b1, :T], self.ones[:, :], cb[c % 2][:, :T], c == 0, c == 7, [("cb", c % 2), "ones"], [("ps", b1)])
                self.mm(PS[:, b2, :T], self.ones[:, :], cq[c % 2][:, :T], c == 0, c == 7, [("cq", c % 2), "ones"], [("ps", b2)])
            for c in range(8):
                dg = Dg[self.ndg % 2]
                dres = ("dg", self.ndg % 2)
                self.ndg += 1
                wtap = self.CV[:, dwc + c:dwc + c + 31 * 8:8].unsqueeze(2).to_broadcast([128, 31, 128])
                idb = identb.unsqueeze(1).to_broadcast([128, 31, 128])
                self.tt(dg[:, :, :], idb, wtap, ALU.mult, ["identb", "cv"], [dres])
                b = self.bank()
                for k in range(31):
                    if not samp:
                        gk = gext[:, c, k:k + T]
                        po = PS[:, b, :T]
                    else:
                        gk = gext[:, c, 0:544].rearrange("p (b r) -> p b r", r=34)[:, :, k:k + 4]
                        po = sview(PS[:, b, :T])
                    self.mm(po, dg[:, k, :], gk, k == 0, k == 30, [dres, gres], [("ps", b)])
                if c >= 1:
                    stats_mm(c - 1)
                self.act(ct[:, c, :T], PS[:, b, :T], AF.Identity, [("ps", b), "cv"], [("ct", c)], bias=self.cv("conv_dw_b", c))
                self.cp("act", cb[c % 2][:, :T], ct[:, c, :T], [("ct", c)], [("cb", c % 2)])
                self.act(cq[c % 2][:, :T], ct[:, c, :T], AF.Square, [("ct", c)], [("cq", c % 2)])
            stats_mm(7)

        def stage_L1(ti):
            t0, t1, T, samp = tinfo(ti)
            b1, b2 = self.lnb12
            self.cp("act", mean_sb[:, :T], PS[:, b1, :T], [("ps", b1)], ["mean"])
            self.act(mv[:, :T], PS[:, b1, :T], AF.Square, [("ps", b1)], ["mv"])
            self.tt(mv[:, :T], PS[:, b2, :T], mv[:, :T], ALU.subtract, [("ps", b2), "mv"], ["mv"])
            self.reserved -= {b1, b2}
            self.act(mv[:, :T], mv[:, :T], AF.Sqrt, ["mv", "eps"], ["mv"], bias=self.eps[:, 0:1])
            self.recip(mv[:, :T], mv[:, :T], ["mv"], ["mv"])
            for c in range(8):
                self.tt(ct[:, c, :T], ct[:, c, :T], mean_sb[:, :T], ALU.subtract, [("ct", c), "mean"], [("ct", c)])
            for c in range(8):
                self.tt(ct[:, c, :T], ct[:, c, :T], mv[:, :T], ALU.mult, [("ct", c), "mv"], [("ct", c)])

        def stage_L2(ti):
            t0, t1, T, samp = tinfo(ti)
            for c in range(8):
                self.act(cn[:, c, :T], ct[:, c, :T], AF.Silu, [("ct", c), "cv"], [("cn", c)],
                         scale=self.cv("conv_ln_g", c), bias=self.cv("conv_ln_b", c))

        def stage_O(ti):
            t0, t1, T, samp = tinfo(ti)
            for m in range(8):
                b = self.bank()
                for k in range(8):
                    self.mm(PS[:, b, :T], wout[m // 4][:, k, (m % 4) * 128:(m % 4 + 1) * 128], cn[:, k, :T], k == 0, k == 7,
                            [("W", 4 + m // 4), ("cn", k)], [("ps", b)])
                self.tt(X[:, m, t0:t1], X[:, m, t0:t1], PS[:, b, :T], ALU.add, [("x", m, ti), ("ps", b)], [("x", m, ti)])

        stage_N(0)
        stage_A(0)
        for ti in range(5):
            nxt = ti + 1 < 5
            stage_C(ti)
            if nxt:
                stage_N(ti + 1)
            stage_L1(ti)
            if nxt:
                stage_A(ti + 1)
            stage_L2(ti)
            stage_O(ti)

    def phase_gmlp(self, part="all"):
        save_off = self.scr_off
        self.scr_reset()
        sq, sd, rstd = self.norm_scratch()
        ut = self.scr(2048, bf=True).rearrange("p (c t) -> p c t", c=8)
        utsp = self.scr(512)
        vg = [self.scr(1024) for _ in range(4)]
        lng = self.scr(1024)
        lnb = self.scr(1024)
        Cb = self.scr(1024).rearrange("p (c t) -> p c t", c=8)
        tmp = [self.scr(512), self.scr(512)]
        stats = self.scr(72)
        ht = self.hreg(0, 2048, bf=True).rearrange("p (c t) -> p c t", c=8)
        yb = self.hreg(2048, 2048, bf=True).rearrange("p (c t) -> p c t", c=8)
        vb = [self.hreg(4096 + i * 2048, 2048, bf=True).rearrange("p (q e) -> p q e", q=4) for i in range(2)]
        wsTb = self.hreg(8448, 256, bf=True).rearrange("p (g t) -> p g t", g=4)
        BDb = self.hreg(8704, 128, bf=True).rearrange("p (g t) -> p g t", g=4)
        CS = self.hreg(8832, 512).rearrange("p (c t) -> p c t", c=8)
        so = O_SCR + 5120
        wstg = self.f32(so, 512).rearrange("p (g s) -> p g s", g=4)
        wsTf = self.f32(so + 512, 512).rearrange("p (g t) -> p g t", g=4)
        tri = self.f32(so + 1024, 128)
        BDr = self.f32(so + 1152, 256).rearrange("p (g t) -> p g t", g=4)
        bsb = self.f32(so + 1408, 512).rearrange("p (g t) -> p g t", g=4)
        bsbS = self.f32(so + 1920, 256).rearrange("p (g t) -> p g t", g=4)
        onesf = self.f32(so + 2176, 128)
        X, PS, dr, S = self.X, self.PS, self.dr, self.S
        def gm_setup():
            self.dma("sp", lng[:, :], dr["gm_ln_g"].to_broadcast([128, D]), [], ["lng"])
            self.dma("sp", lnb[:, :], dr["gm_ln_b"].to_broadcast([128, D]), [], ["lnb"])
            for g in range(4):
                self.dma("sp", bsb[:, g, :], dr["gm_b_s"][g:g + 1, :].to_broadcast([128, 128]), [], [("bsb", g)])
            self.dma("sp", wstg, dr["gm_w_s"].rearrange("g t s -> t g s"), [], ["wstg"])
            self.memset("pool", tri[:, :], 1.0, ["tri"])
            S.op("pool", lambda e: e.affine_select(out=tri[:, :], in_=tri[:, :], pattern=[[1, 128]], compare_op=ALU.is_ge,
                                                   fill=0.0, base=0, channel_multiplier=-1), ["tri"], ["tri"])
            self.memset("pool", onesf[:, :], 1.0, ["onesf"])
            b = self.bank()
            for g in range(4):
                self.tr(PS[:, b, g * 128:(g + 1) * 128], wstg[:, g, :], self.ident[:, :], ["wstg", "ident"], [("ps", b)])
            for g in range(4):
                self.tt(wsTf[:, g, :], PS[:, b, g * 128:(g + 1) * 128], tri[:, :], ALU.mult, [("ps", b), "tri"], ["wsTf"])
            self.cp("act", wsTb[:, :, :], wsTf[:, :, :], ["wsTf"], ["wsTb"])
            self.memset("pool", BDr[:, :, :], 0.0, ["BDr"])
            for bq in range(NSEQ):
                self.dma("sp", BDr[4 * bq:4 * bq + 4, :, 4 * bq:4 * bq + 4], wsTf[0:4, :, 0:4], ["wsTf", "BDr"], [("BDr", bq)])
            bdres = [("BDr", bq) for bq in range(NSEQ)]
            self.cp("act", BDb[:64, :, :], BDr[:64, :, :], bdres, ["BDb"])
            for g in range(4):
                self.cp("pool", bsbS[:, g, :].rearrange("p (b t) -> p b t", t=4), bsb[:, g, 0:4].unsqueeze(1).to_broadcast([128, 16, 4]),
                        [("bsb", g)], ["bsbS"])
            b2 = self.bank()
            for g in range(4):
                self.mm(PS[:, b2, g * 128:(g + 1) * 128], onesf[:, :], wsTf[:, g, :], True, True, ["onesf", "wsTf"], [("ps", b2)])
            for cc in range(8):
                g = cc // 2
                self.stt(Cb[:, cc, :], PS[:, b2, g * 128:(g + 1) * 128], self.cv("gm_ln_b", cc), bsb[:, g, :], ALU.mult, ALU.add,
                         [("ps", b2), "cv", ("bsb", g)], ["Cb"])
            b3 = self.bank()
            for g in range(4):
                self.mm(PS[:, b3, g * 64:(g + 1) * 64], onesf[:64, :], BDr[:64, g, :], True, True, ["onesf"] + bdres, [("ps", b3)])
            for cc in range(8):
                g = cc // 2
                self.stt(CS[:, cc, :], PS[:, b3, g * 64:(g + 1) * 64], self.cv("gm_ln_b", cc), bsbS[:, g, :], ALU.mult, ALU.add,
                         [("ps", b3), "cv", "bsbS"], ["CS"])

        if part in ("all", "setup") and not getattr(self, "gm_setup_done", False):
            self.gm_setup_done = True
            gm_setup()
        if part == "setup":
            self.scr_off = save_off
            return
        win = [self.mixer_win("gm_w_in", j) for j in range(4)]
        wout = [self.wunit(4 + j).rearrange("p (k f) -> p k f", k=8) for j in range(2)]
        for j in range(2):
            self.load_wblock(wout[j], dr["gm_w_out"], j * 512, 512, ("W", 4 + j))
        if part == "all":
            S.barrier()

        def tinfo(ti):
            t0, t1 = TILES[ti]
            return t0, t1, t1 - t0, ti == 4

        def stage_norm(ti):
            stage_norm_a(ti)
            stage_norm_b(ti)

        def stage_norm_a(ti):
            self.norm_a(ti, sq, sd, rstd)

        def stage_norm_b(ti):
            t0, t1, T, samp = tinfo(ti)
            self.norm_b(ti, "norm_mix", 16, lambda c, T=T: (ht[:, c, :T], ("ht", c), None), rstd)

        def stage_U(ti):
            t0, t1, T, samp = tinfo(ti)
            for m in range(8):
                b = self.bank()
                for k in range(8):
                    self.mm(PS[:, b, :T], win[m // 4][:, k, (m % 4) * 128:(m % 4 + 1) * 128], ht[:, k, :T], k == 0, k == 7,
                            [("W", m // 4), ("ht", k)], [("ps", b)])
                self.act(ut[:, m, :T], PS[:, b, :T], AF.Gelu_apprx_tanh, [("ps", b)], [("ut", m)])

        def stage_V(ti):
            t0, t1, T, samp = tinfo(ti)
            nq = 1 if samp else 4
            R = 64 if samp else 128
            for q in range(nq):
                v = vg[q]
                for eh in range(2):
                    b = self.bank()
                    for k in range(8):
                        self.mm(PS[:R, b, :], ht[:, k, q * 128:q * 128 + R], win[2 + eh][:, k, :], k == 0, k == 7,
                                [("W", 2 + eh), ("ht", k)], [("ps", b)])
                    self.act(v[:R, eh * 512:(eh + 1) * 512], PS[:R, b, :], AF.Gelu_apprx_tanh, [("ps", b)], [("vg", q)])

        def stage_LN(ti):
            t0, t1, T, samp = tinfo(ti)
            nq = 1 if samp else 4
            R = 64 if samp else 128
            vbt = vb[ti % 2]
            st6 = stats[:, 0:48].rearrange("p (q a s) -> p q a s", q=4, a=2)
            mv = stats[:, 48:56].rearrange("p (q s) -> p q s", q=4)
            for q in range(nq):
                for eh in range(2):
                    S.op("dve", (lambda q=q, eh=eh, R=R: (lambda e: e.bn_stats(out=st6[:R, q, eh, :], in_=vg[q][:R, eh * 512:(eh + 1) * 512])))(),
                         [("vg", q)], [("s6", q, eh)])
                S.op("dve", (lambda q=q, R=R: (lambda e: e.bn_aggr(out=mv[:R, q, :], in_=st6[:R, q, :, :].rearrange("p a s -> p (a s)"))))(),
                     [("s6", q, 0), ("s6", q, 1)], ["mv"])
            sd4, rs4, nb4 = stats[:, 56:60], stats[:, 60:64], stats[:, 64:68]
            self.act(sd4[:R, :nq], mv[:R, :nq, 1], AF.Sqrt, ["mv", "eps"], ["sd4"], bias=self.eps[:R, 0:1])
            self.recip(rs4[:R, :nq], sd4[:R, :nq], ["sd4"], ["rs4"])
            self.stt(nb4[:R, :nq], mv[:R, :nq, 0], -1.0, rs4[:R, :nq], ALU.mult, ALU.mult, ["mv", "rs4"], ["nb4"])
            for q in range(nq):
                v = vg[q]
                self.act(vbt[:R, q, :], v[:R, :], AF.Identity, [("vg", q), "rs4", "nb4"], [("vb", ti % 2, q)],
                         scale=rs4[:R, q:q + 1], bias=nb4[:R, q:q + 1])
                if samp or (ti == 3 and q == 3):
                    self.stt(v[:R, :], v[:R, :], mv[:R, q, 0:1], lng[:R, :], ALU.subtract, ALU.mult, [("vg", q), "mv", "lng"], [("vg", q)])
                    self.stt(v[:R, :], v[:R, :], rs4[:R, q:q + 1], lnb[:R, :], ALU.mult, ALU.add, [("vg", q), "rs4", "lnb"], [("vg", q)])
                    if samp:
                        self.dma("sp", dr["v_s"], v[:64, :], [("vg", q)], [])
                    else:
                        self.dma("sp", dr["v_p"], v[:, :], [("vg", q)], [])

        def stage_S(ti):
            t0, t1, T, samp = tinfo(ti)
            vbt = vb[ti % 2]
            for cc in range(8):
                g = cc // 2
                b = self.bank()
                if not samp:
                    for q in range(4):
                        self.mm(PS[:, b, q * 128:(q + 1) * 128], vbt[:, q, cc * 128:(cc + 1) * 128], wsTb[:, g, :], True, True,
                                [("vb", ti % 2, q), "wsTb"], [("ps", b)])
                    self.stt(tmp[cc % 2][:, :T].rearrange("p (q t) -> p q t", q=4), PS[:, b, :T].rearrange("p (q t) -> p q t", q=4),
                             self.cv("gm_ln_g", cc), Cb[:, cc, :].unsqueeze(1).to_broadcast([128, 4, 128]), ALU.mult, ALU.add,
                             [("ps", b), "cv", "Cb"], [("tmp", cc % 2)])
                else:
                    self.mm(PS[:, b, :64], vbt[:64, 0, cc * 128:(cc + 1) * 128], BDb[:64, g, :], True, True, [("vb", ti % 2, 0), "BDb"], [("ps", b)])
                    self.stt(tmp[cc % 2][:, :T], PS[:, b, :T], self.cv("gm_ln_g", cc), CS[:, cc, :], ALU.mult, ALU.add,
                             [("ps", b), "cv", "CS"], [("tmp", cc % 2)])
                self.tt(yb[:, cc, :T], tmp[cc % 2][:, :T], ut[:, cc, :T], ALU.mult, [("tmp", cc % 2), ("ut", cc)], [("yb", cc)])

        def stage_O(ti):
            t0, t1, T, samp = tinfo(ti)
            for m in range(8):
                b = self.bank()
                for k in range(8):
                    self.mm(PS[:, b, :T], wout[m // 4][:, k, (m % 4) * 128:(m % 4 + 1) * 128], yb[:, k, :T], k == 0, k == 7,
                            [("W", 4 + m // 4), ("yb", k)], [("ps", b)])
                self.tt(X[:, m, t0:t1], X[:, m, t0:t1], PS[:, b, :T], ALU.add, [("x", m, ti), ("ps", b)], [("x", m, ti)])

        stage_norm(0)
        stage_V(0)
        stage_LN(0)
        stage_U(0)
        stage_norm(1)
        for ti in range(5):
            nxt = ti + 1 < 5
            if nxt:
                stage_V(ti + 1)
            stage_S(ti)
            if ti + 2 < 5:
                stage_norm_a(ti + 2)
            if nxt:
                stage_LN(ti + 1)
                stage_U(ti + 1)
            if ti + 2 < 5:
                stage_norm_b(ti + 2)
            if ti == 3:
                self.ffn_load(self.cur_L, 0, self.cur_j)
            stage_O(ti)

    def phase_sc(self):
        self.scr_reset()
        sq, sd, rstd = self.norm_scratch()
        cxe = self.scr(8 * 514).rearrange("p (c t) -> p c t", c=8)
        acc = [self.scr(512) for _ in range(2)]
        stg = self.scr(1024)
        csm = self.scr(256).rearrange("p (c t) -> p c t", c=8)
        ht = self.hreg(0, 2048, bf=True).rearrange("p (c t) -> p c t", c=8)
        ybs = [self.hreg(2048, 2048, bf=True).rearrange("p (c t) -> p c t", c=8),
               self.scr(2048, bf=True).rearrange("p (c t) -> p c t", c=8)]
        wout = [self.hreg(4096 + j * 2048, 2048, bf=True).rearrange("p (k f) -> p k f", k=8) for j in range(2)]
        cgs = [self.hreg(8192 + i * 512, 512) for i in range(2)]
        bbs = [self.scr(512) for _ in range(2)]
        X, PS, dr = self.X, self.PS, self.dr
        win = [self.mixer_win("sc_w_in", j) for j in range(6)]
        for j in range(2):
            self.load_wblock(wout[j], dr["sc_w_out"], j * 512, 512, ("scwo", j))
        sview = lambda ap: ap.rearrange("p (b t) -> p b t", t=4)

        def front(ti):
            t0, t1 = TILES[ti]
            T = t1 - t0
            samp = ti == 4
            yb = ybs[ti % 2]
            if ti == 0:
                self.memset("pool", cxe[:, :, 0:2], 0.0, ["cxe"])
            elif not samp:
                self.cp("pool", cxe[:, :, 0:2], cxe[:, :, 512:514], ["cxe"], ["cxe"])
            else:
                def dst(c, ps, psres):
                    o = cxe[:, c, 0:96].rearrange("p (b r) -> p b r", r=6)[:, :, 0:2]
                    self.cp("act", o, ps.rearrange("p (b r) -> p b r", r=2), [psres], ["cxe"])
                self.load_T(dr["ssc"], 32, dst, None, stg, "stg")
            for m in range(8):
                bb, bc, bx = self.bank(), self.bank(), self.bank()
                for (bk, j0) in ((bb, 0), (bc, 2), (bx, 4)):
                    for k in range(8):
                        self.mm(PS[:, bk, :T], win[j0 + m // 4][:, k, (m % 4) * 128:(m % 4 + 1) * 128], ht[:, k, :T], k == 0, k == 7,
                                [("W", j0 + m // 4), ("ht", k)], [("ps", bk)])
                cg = cgs[m % 2]
                self.cp("act", cg[:, :T], PS[:, bc, :T], [("ps", bc)], [("cgs", m % 2)])
                bsb_ = bbs[m % 2]
                self.cp("act", bsb_[:, :T], PS[:, bb, :T], [("ps", bb)], [("bbs", m % 2)])
                a = acc[m % 2]
                ares = ("acc", m % 2)
                if not samp:
                    self.tt(cxe[:, m, 2:2 + T], cg[:, :T], PS[:, bx, :T], ALU.mult, [("cgs", m % 2), ("ps", bx), "cxe"], ["cxe", ("cx", m)])
                    sl = lambda k, m=m, T=T: cxe[:, m, k:k + T]
                    av = a[:, :T]
                    pb = bsb_[:, :T]
                    yo = yb[:, m, :T]
                else:
                    ev = cxe[:, m, 0:96].rearrange("p (b r) -> p b r", r=6)
                    self.tt(ev[:, :, 2:6], sview(cg[:, :T]), sview(PS[:, bx, :T]), ALU.mult, [("cgs", m % 2), ("ps", bx), "cxe"], ["cxe", ("cx", m)])
                    sl = lambda k, ev=ev: ev[:, :, k:k + 4]
                    av = sview(a[:, :T])
                    pb = sview(bsb_[:, :T])
                    yo = sview(yb[:, m, :T])
                self.ts(av, sl(0), self.cv("sc_conv", m), None, ALU.mult, None, ["cxe", "cv"], [ares])
                self.stt(av, sl(1), self.cv("sc_conv", 8 + m), av, ALU.mult, ALU.add, ["cxe", "cv", ares], [ares])
                self.stt(av, sl(2), self.cv("sc_conv", 16 + m), av, ALU.mult, ALU.add, ["cxe", "cv", ares], [ares])
                self.tt(yo, av, pb, ALU.mult, [ares, ("bbs", m % 2)], [("yb", ti % 2, m)])
            if ti == 3:
                self.store_T(lambda c: cxe[:, c, 512:514], ["cxe"], 2,
                             lambda s_, r_: self.dma("sp", dr["sc_p"], s_[:2, :], [r_], []), stg, "stg")
            if samp:
                for c in range(8):
                    self.cp("act", csm[:, c, :].rearrange("p (b t) -> p b t", t=2),
                            cxe[:, c, 0:96].rearrange("p (b r) -> p b r", r=6)[:, :, 4:6], ["cxe"], [("csm", c)])
                self.store_T(lambda c: csm[:, c, :], [("csm", c) for c in range(8)], 32,
                             lambda s_, r_: self.dma("sp", dr["sc_s"], s_[:32, :], [r_], []), stg, "stg")
                self.ffn_load(self.cur_L, 0, self.cur_j)

        def back(ti):
            t0, t1 = TILES[ti]
            T = t1 - t0
            yb = ybs[ti % 2]
            for m in range(8):
                b = self.bank()
                for k in range(8):
                    self.mm(PS[:, b, :T], wout[m // 4][:, k, (m % 4) * 128:(m % 4 + 1) * 128], yb[:, k, :T], k == 0, k == 7,
                            [("scwo", m // 4), ("yb", ti % 2, k)], [("ps", b)])
                self.tt(X[:, m, t0:t1], X[:, m, t0:t1], PS[:, b, :T], ALU.add, [("x", m, ti), ("ps", b)], [("x", m, ti)])

        def snorm(ti):
            T = TILES[ti][1] - TILES[ti][0]
            self.norm(ti, "norm_mix", 24, lambda c, T=T: (ht[:, c, :T], ("ht", c), None), sq, sd, rstd)

        snorm(0)
        for ti in range(5):
            front(ti)
            if ti + 1 < 5:
                snorm(ti + 1)
            if ti >= 1:
                back(ti - 1)
        back(4)

    def run(self):
        import os
        S = self.S
        stop = _dev("KPH", "all")
        self.setup_consts()
        S.barrier()
        if stop != "consts":
            self.phase_input()
            S.barrier()
        j = 0
        names = ("pool", "conv", "gmlp", "sc")
        wnames = (None, "conv_w_in", "gm_w_in", "sc_w_in")
        skip = _dev("KSKIP", "").split(",")
        for L, ph in enumerate((self.phase_pool, self.phase_conv, self.phase_gmlp, self.phase_sc)):
            if stop in ("input", "consts"):
                break
            self.cur_L, self.cur_j = L, j
            if names[L] not in skip:
                if names[L] == "gmlp" and getattr(self, "gm_setup_done", False):
                    ph("main")
                else:
                    ph()
                S.barrier()
            if stop == names[L]:
                break
            if "ffn" not in skip:
                nxt = None
                if L + 1 < 4:
                    nxt = (lambda nm=wnames[L + 1]: [self.mixer_win(nm, jj) for jj in range(3)])
                pre = (lambda: self.phase_gmlp("setup")) if (L == 1 and "gmlp" not in skip and stop not in ("ffn1",)) else None
                j = self.phase_ffn(L, j, nxt, pre)
                S.barrier()
            if stop == "ffn%d" % L:
                break
        if _dev("KNOFINAL") is None:
            self.phase_final()


_NC = None


def build():
    nc = bass.Bass("TRN2", target_bir_lowering=False)
    dr = {}
    for n in IN_NAMES:
        dr[n] = nc.dram_tensor(n, IN_SHAPES[n], F32, kind="ExternalInput").ap()
    for n, s in OUT_SHAPES.items():
        dr[n] = nc.dram_tensor(n, s, F32, kind="ExternalOutput").ap()
    S = Sched()
    with ExitStack() as st:
        A = st.enter_context(nc.sbuf_tensor("arena", [128, NW], F32))
        PS = st.enter_context(nc.psum_tensor("ps", [128, 8, 512], F32))
        k = Kern(nc, S, A, PS, dr)
        k.run()
        S.emit(nc, st)
    return nc


def kernel(**inp):
    global _NC
    if _NC is None:
        _NC = build()
    nc = _NC
    f = lambda a: np.ascontiguousarray(np.asarray(a, dtype=np.float32))
    shared = {}
    for n in IN_NAMES[5:]:
        shared[n] = f(inp[n]).reshape(IN_SHAPES[n])
    xp, xs = f(inp["x_prompt"]), f(inp["x_sample"])
    sp_, sc_, ss_ = f(inp["state_pool"]), f(inp["state_conv"]), f(inp["state_shortconv"])
    in_maps = []
    for c in range(8):
        m = dict(shared)
        sl = slice(NSEQ * c, NSEQ * (c + 1))
        m["xp"] = xp[c]
        m["xs"] = np.ascontiguousarray(xs[sl].reshape(64, D))
        m["spool"] = np.ascontiguousarray(sp_[0, sl].reshape(240, D))
        m["sconv"] = np.ascontiguousarray(sc_[0, sl].reshape(480, D))
        m["ssc"] = np.ascontiguousarray(ss_[0, sl].reshape(32, D))
        in_maps.append(m)
    import os
    ncores = int(_dev("KCORES", "8"))
    res = run_bass_kernel_spmd(nc, in_maps[:ncores], core_ids=list(range(ncores)))
    R = list(res.results)
    while len(R) < 8:
        R.append(R[0])
    cat = lambda n, shp: np.stack([np.asarray(R[c][n], dtype=np.float32).reshape(shp) for c in range(8)])
    y_p = cat("yp", (SEQ, D))
    y_s = cat("ys", (NSEQ, DEC, D)).reshape(128, DEC, D)
    pool_p = cat("pool_p", (15, D))[None]
    pool_s = cat("pool_s", (NSEQ, 15, D)).reshape(128, 15, D)[None]
    conv_p = cat("conv_p", (30, D))[None]
    conv_s = cat("conv_s", (NSEQ, 30, D)).reshape(128, 30, D)[None]
    v_p = cat("v_p", (128, D))[None]
    v_s = cat("v_s", (NSEQ, DEC, D)).reshape(128, DEC, D)[None]
    sc_p = cat("sc_p", (2, D))[None]
    sc_s = cat("sc_s", (NSEQ, 2, D)).reshape(128, 2, D)[None]
    return (y_p, y_s, pool_p, pool_s, conv_p, conv_s, v_p, v_s, sc_p, sc_s)
```

```python
from contextlib import ExitStack
import numpy as np
import concourse.bass as bass
import concourse.mybir as mybir
from concourse.bass_utils import run_bass_kernel_spmd
import os as _os


def _dev(key, default=None):
    return _os.environ.get(key, default) if _os.environ.get("MK_DEV") == "1" else default

F32 = mybir.dt.float32
BF16 = mybir.dt.bfloat16
ALU = mybir.AluOpType
AF = mybir.ActivationFunctionType

D = 1024
SEQ = 2048
NSEQ = 16
DEC = 4
NTOK = SEQ + NSEQ * DEC
DFF = 2816
TILES = [(0, 512), (512, 1024), (1024, 1536), (1536, 2048), (2048, 2112)]
FTILES = [(0, 448), (448, 896), (896, 1344), (1344, 1792), (1792, 2112)]
EPS = 1e-6


class Sched:
    ENGS = ("pe", "act", "dve", "pool", "sp")
    NS = 12

    def __init__(self):
        self.ops = {e: [] for e in self.ENGS}
        self.res = {}
        self.ndma = {e: 0 for e in self.ENGS}
        import os
        self.same_skip = int(_dev('KSAMESKIP', '1000000000'))

    def op(self, eng, fn, reads=(), writes=(), dma=False):
        deps = set()
        idx = len(self.ops[eng])
        if dma:
            ref = ("d", eng, self.ndma[eng])
            self.ndma[eng] += 1
        else:
            ref = ("c", eng, idx)
        for r in reads:
            st = self.res.get(r)
            if st is not None and st[0] is not None:
                deps.add(st[0])
        for r in writes:
            st = self.res.get(r)
            if st is not None:
                if st[0] is not None:
                    deps.add(st[0])
                deps.update(st[1])
        for r in reads:
            st = self.res.setdefault(r, [None, []])
            st[1].append(ref)
        for r in writes:
            self.res[r] = [ref, []]
        deps.discard(ref)
        keep = set()
        for d in deps:
            if d[0] == "c" and d[1] == eng and not dma:
                if eng == "pe":
                    continue
                if idx - d[2] > self.same_skip:
                    continue
            keep.add(d)
        self.ops[eng].append(dict(fn=fn, deps=keep, dma=dma, ref=ref, signal=False, count=None, tag=(tuple(reads), tuple(writes))))
        return ref

    def barrier(self):
        last = {}
        for e in ("pe", "act", "dve", "pool"):
            for i in range(len(self.ops[e]) - 1, -1, -1):
                o = self.ops[e][i]
                if not o["dma"] and o["fn"] is not None:
                    last[e] = ("c", e, i)
                    break
        dmadeps = set()
        for q in self.ENGS:
            n = self.ndma[q]
            for i in range(max(0, n - self.NS), n):
                dmadeps.add(("d", q, i))
        for e in self.ENGS:
            deps = set(v for k, v in last.items() if k != e) | dmadeps
            self.ops[e].append(dict(fn=None, deps=deps, dma=False, ref=("c", e, len(self.ops[e])), signal=False, count=None))
        self.res = {}

    def finalize(self):
        for e in self.ENGS:
            for o in self.ops[e]:
                for d in o["deps"]:
                    if d[0] == "c":
                        self.ops[d[1]][d[2]]["signal"] = True
        for e in self.ENGS:
            c = 0
            for o in self.ops[e]:
                if o["signal"] and not o["dma"]:
                    c += 1
                    o["count"] = c

    def emit(self, nc, stack):
        self.finalize()
        csem = {e: stack.enter_context(nc.semaphore("s_" + e)) for e in ("pe", "act", "dve", "pool")}
        dsem = {}
        for q in self.ENGS:
            if self.ndma[q] > 0:
                dsem[q] = [stack.enter_context(nc.semaphore("d_%s%d" % (q, i))) for i in range(min(self.NS, self.ndma[q]))]
        block = stack.enter_context(nc.Block())
        import os
        DUMP = open(_dev("KDUMP"), "w") if _dev("KDUMP") else None
        semname = {id(v): "s_" + k for k, v in csem.items()}
        for q in dsem:
            for i, v in enumerate(dsem[q]):
                semname[id(v)] = "d_%s%d" % (q, i)
        NS = self.NS
        sched = self

        def run(eng_name, eng):
            waited = {}

            def wait(sem, val):
                k = id(sem)
                if waited.get(k, 0) >= val:
                    return
                waited[k] = val
                if DUMP is not None:
                    DUMP.write("%s   wait %s >= %s\n" % (eng_name, semname[k], val))
                eng.wait_ge(sem, val)

            for o in sched.ops[eng_name]:
                for d in sorted(o["deps"]):
                    if d[0] == "c":
                        wait(csem[d[1]], sched.ops[d[1]][d[2]]["count"])
                    else:
                        q, i = d[1], d[2]
                        wait(dsem[q][i % NS], 16 * (i // NS + 1))
                if o["fn"] is None:
                    continue
                if o["dma"]:
                    i = o["ref"][2]
                    if i >= NS:
                        wait(dsem[eng_name][i % NS], 16 * (i // NS))
                inst = o["fn"](eng)
                if DUMP is not None:
                    DUMP.write("%s op %s %s sig=%s cnt=%s\n" % (eng_name, o["ref"], o.get("tag"), o["signal"], o["count"]))
                if o["dma"]:
                    i = o["ref"][2]
                    inst.then_inc(dsem[eng_name][i % NS], 16)
                elif o["signal"]:
                    inst.then_inc(csem[eng_name], 1)
            if eng_name == "sp":
                for q in dsem:
                    n = sched.ndma[q]
                    for s in range(len(dsem[q])):
                        cnt = (n - s + NS - 1) // NS
                        if cnt > 0:
                            wait(dsem[q][s], 16 * cnt)

        @block.tensor
        def _(e):
            run("pe", e)

        @block.scalar
        def _(e):
            run("act", e)

        @block.vector
        def _(e):
            run("dve", e)

        @block.gpsimd
        def _(e):
            run("pool", e)

        @block.sync
        def _(e):
            run("sp", e)


NW = 53100
O_X = 0
O_W = O_X + 8 * NTOK
WU = 2048
O_CV = O_W + 6 * WU
O_ID = O_CV + 512
O_ONES = O_ID + 128
O_EPS = O_ONES + 64
O_INVC = O_EPS + 4
O_H = O_INVC + 64
HW_ = 9472
O_SCR = O_H + HW_
SCRW = NW - O_SCR

IN_NAMES = ["xp", "xs", "spool", "sconv", "ssc", "norm_mix", "norm_ffn", "norm_final", "pool_w", "pool_scale",
            "conv_w_in", "conv_b_in", "conv_dw", "conv_dw_b", "conv_ln_g", "conv_ln_b", "conv_w_out",
            "gm_w_in", "gm_ln_g", "gm_ln_b", "gm_w_s", "gm_b_s", "gm_w_out",
            "sc_w_in", "sc_conv", "sc_w_out", "ffn_w_in", "ffn_w_out"]
IN_SHAPES = {
    "xp": [SEQ, D], "xs": [64, D], "spool": [240, D], "sconv": [480, D], "ssc": [32, D],
    "norm_mix": [4, D], "norm_ffn": [4, D], "norm_final": [1, D], "pool_w": [4, 256, 256], "pool_scale": [1, D],
    "conv_w_in": [D, 2 * D], "conv_b_in": [1, 2 * D], "conv_dw": [31, D], "conv_dw_b": [1, D],
    "conv_ln_g": [1, D], "conv_ln_b": [1, D], "conv_w_out": [D, D],
    "gm_w_in": [D, 2 * D], "gm_ln_g": [1, D], "gm_ln_b": [1, D], "gm_w_s": [4, 128, 128], "gm_b_s": [4, 128],
    "gm_w_out": [D, D], "sc_w_in": [D, 3 * D], "sc_conv": [3, D], "sc_w_out": [D, D],
    "ffn_w_in": [4, D, 2 * DFF], "ffn_w_out": [4, DFF, D],
}
OUT_SHAPES = {
    "yp": [SEQ, D], "ys": [64, D], "pool_p": [15, D], "pool_s": [240, D], "conv_p": [30, D], "conv_s": [480, D],
    "v_p": [128, D], "v_s": [64, D], "sc_p": [2, D], "sc_s": [32, D],
}


class Kern:
    def __init__(self, nc, S, A, PS, dr):
        self.nc, self.S, self.A, self.PS, self.dr = nc, S, A, PS, dr
        self.nbank = 0
        self.X = self.f32(O_X, 8 * NTOK).rearrange("p (c t) -> p c t", c=8)
        self.CV = self.f32(O_CV, 512)
        self.ident = self.f32(O_ID, 128)
        self.ones = self.b16(O_ONES, 64)
        self.eps = self.f32(O_EPS, 4)
        self.invc = self.f32(O_INVC, 64).rearrange("p (g t) -> p g t", g=4)
        self.cvcol = {}
        self.scr_off = 0
        self.stg_i = 0
        self.pref = set()
        self.reserved = set()

    def f32(self, off, n):
        return self.A[:, off:off + n]

    def b16(self, off, n):
        return self.A[:, off:off + n].bitcast(BF16)

    def scr_reset(self):
        self.scr_off = 0

    def scr(self, n, bf=False):
        off = O_SCR + self.scr_off
        self.scr_off += n
        assert self.scr_off <= SCRW, (self.scr_off, SCRW)
        return self.b16(off, n) if bf else self.f32(off, n)

    def hreg(self, off, n, bf=False):
        assert off + n <= HW_
        return self.b16(O_H + off, n) if bf else self.f32(O_H + off, n)

    def wunit(self, u, nu=1):
        return self.b16(O_W + u * WU, nu * WU)

    def bank(self):
        while True:
            b = self.nbank % 8
            self.nbank += 1
            if b not in self.reserved:
                return b

    def mm(self, out, lhsT, rhs, start, stop, r, w):
        self.S.op("pe", lambda e: e.matmul(out, lhsT=lhsT, rhs=rhs, start=start, stop=stop), r, w)

    def tr(self, out, in_, ident, r, w):
        self.S.op("pe", lambda e: e.transpose(out=out, in_=in_, identity=ident), r, w)

    def act(self, out, in_, func, r, w, bias=None, scale=None, accum_out=None):
        kw = {}
        if bias is not None:
            kw["bias"] = bias
        if scale is not None:
            kw["scale"] = scale
        if accum_out is not None:
            kw["accum_out"] = accum_out
        self.S.op("act", lambda e: e.activation(out=out, in_=in_, func=func, **kw), r, w)

    def tt(self, out, in0, in1, op, r, w, eng="dve"):
        self.S.op(eng, lambda e: e.tensor_tensor(out=out, in0=in0, in1=in1, op=op), r, w)

    def ts(self, out, in0, s1, s2, op0, op1, r, w, eng="dve"):
        if op1 is None:
            self.S.op(eng, lambda e: e.tensor_scalar(out=out, in0=in0, scalar1=s1, scalar2=None, op0=op0), r, w)
        else:
            self.S.op(eng, lambda e: e.tensor_scalar(out=out, in0=in0, scalar1=s1, scalar2=s2, op0=op0, op1=op1), r, w)

    def stt(self, out, in0, scalar, in1, op0, op1, r, w, accum_out=None):
        if accum_out is None:
            self.S.op("dve", lambda e: e.scalar_tensor_tensor(out=out, in0=in0, scalar=scalar, in1=in1, op0=op0, op1=op1), r, w)
        else:
            self.S.op("dve", lambda e: e.scalar_tensor_tensor(out=out, in0=in0, scalar=scalar, in1=in1, op0=op0, op1=op1, accum_out=accum_out), r, w)

    def cp(self, eng, out, in_, r, w):
        if eng == "act":
            self.S.op("act", lambda e: e.copy(out=out, in_=in_), r, w)
        else:
            self.S.op(eng, lambda e: e.tensor_copy(out=out, in_=in_), r, w)

    def recip(self, out, in_, r, w):
        self.S.op("dve", lambda e: e.reciprocal(out=out, in_=in_), r, w)

    def memset(self, eng, ap, val, w):
        self.S.op(eng, lambda e: e.memset(ap, val), [], w)

    def dma(self, q, out, in_, r, w, **kw):
        self.S.op(q, lambda e: e.dma_start(out=out, in_=in_, **kw), r, w, dma=True)

    def load_T(self, rows_ap, R, dst_fn, dst_res, stg, stg_res, eng_sel=0):
        self.dma("sp", stg[:R, :], rows_ap, [], [stg_res])
        for half in range(2):
            b = self.bank()
            for cc in range(4):
                c = half * 4 + cc
                self.tr(self.PS[:, b, cc * 128:cc * 128 + R], stg[:R, c * 128:(c + 1) * 128], self.ident[:R, :R],
                        [stg_res, "ident"], [("ps", b)])
            for cc in range(4):
                c = half * 4 + cc
                dst_fn(c, self.PS[:, b, cc * 128:cc * 128 + R], ("ps", b))

    def store_T(self, src_fn, src_res, R, rows_dma_fn, stg, stg_res):
        for half in range(2):
            b = self.bank()
            for cc in range(4):
                c = half * 4 + cc
                self.tr(self.PS[:R, b, cc * 128:(cc + 1) * 128], src_fn(c), self.ident[:, :], list(src_res) + ["ident"], [("ps", b)])
            eng = "act" if half == 0 else "dve"
            self.cp(eng, stg[:R, half * 512:(half + 1) * 512], self.PS[:R, b, :], [("ps", b)], [stg_res])
        rows_dma_fn(stg, stg_res)

    def setup_consts(self):
        S = self.S
        ident = self.ident
        self.memset("pool", ident[:, :], 0.0, ["ident"])
        S.op("pool", lambda e: e.affine_select(out=ident[:, :], in_=ident[:, :], pattern=[[-1, 128]], compare_op=ALU.not_equal,
                                               fill=1.0, base=0, channel_multiplier=1), ["ident"], ["ident"])
        self.memset("pool", self.ones[:, :], 1.0 / 1024.0, ["ones"])
        self.memset("pool", self.eps[:, :], EPS, ["eps"])
        for g, win in enumerate((2, 4, 8, 16)):
            self.memset("pool", self.invc[:, g, :], 1.0 / win, ["invc"])
            for t in range(win - 1):
                self.memset("pool", self.invc[:, g, t:t + 1], 1.0 / (t + 1), ["invc"])
        dr = self.dr
        vecs = [("norm_mix", dr["norm_mix"].rearrange("l (c p) -> (l c) p", p=128), 32),
                ("norm_ffn", dr["norm_ffn"].rearrange("l (c p) -> (l c) p", p=128), 32),
                ("norm_final", dr["norm_final"].rearrange("l (c p) -> (l c) p", p=128), 8),
                ("pool_scale", dr["pool_scale"].rearrange("l (c p) -> (l c) p", p=128), 8),
                ("conv_b_in", dr["conv_b_in"].rearrange("l (c p) -> (l c) p", p=128), 16),
                ("conv_dw", dr["conv_dw"].rearrange("l (c p) -> (l c) p", p=128), 248),
                ("conv_dw_b", dr["conv_dw_b"].rearrange("l (c p) -> (l c) p", p=128), 8),
                ("conv_ln_g", dr["conv_ln_g"].rearrange("l (c p) -> (l c) p", p=128), 8),
                ("conv_ln_b", dr["conv_ln_b"].rearrange("l (c p) -> (l c) p", p=128), 8),
                ("sc_conv", dr["sc_conv"].rearrange("l (c p) -> (l c) p", p=128), 24),
                ("gm_ln_g", dr["gm_ln_g"].rearrange("l (c p) -> (l c) p", p=128), 8),
                ("gm_ln_b", dr["gm_ln_b"].rearrange("l (c p) -> (l c) p", p=128), 8)]
        self.scr_reset()
        stg = self.scr(4 * 128).rearrange("p (j q) -> p j q", j=4)
        row = 0
        stres = {j: [] for j in range(4)}
        for name, ap, n in vecs:
            self.cvcol[name] = row
            done = 0
            while done < n:
                j, r0 = divmod(row + done, 128)
                k = min(n - done, 128 - r0)
                rn = ("cvstg", j, r0)
                stres[j].append(rn)
                self.dma("sp", stg[r0:r0 + k, j, :], ap[done:done + k, :], [], [rn])
                done += k
            row += n
        assert row <= 512
        ntile = (row + 127) // 128
        b = self.bank()
        for j in range(ntile):
            R = min(128, row - j * 128)
            self.tr(self.PS[:, b, j * 128:j * 128 + R], stg[:R, j, :], self.ident[:R, :R], stres[j] + ["ident"], [("ps", b)])
        self.cp("dve", self.CV[:, :row], self.PS[:, b, :row], [("ps", b)], ["cv"])

    def cv(self, name, i):
        c = self.cvcol[name] + i
        return self.CV[:, c:c + 1]

    def norm_a(self, ti, sq, sd, rstd, tiles=None):
        t0, t1 = (tiles or TILES)[ti]
        T = t1 - t0
        X = self.X
        b = self.bank()
        for c in range(8):
            self.act(sq[:, c % 4, :T], X[:, c, t0:t1], AF.Square, [("x", c, ti)], [("sq", c % 4)])
            self.mm(self.PS[:, b, :T], self.ones[:, :], sq[:, c % 4, :T], c == 0, c == 7, [("sq", c % 4), "ones"], [("ps", b)])
        self.act(sd[:, :T], self.PS[:, b, :T], AF.Sqrt, [("ps", b), "eps"], ["sd"], bias=self.eps[:, 0:1])
        self.recip(rstd[:, :T], sd[:, :T], ["sd"], ["rstd"])

    def norm_b(self, ti, gname, gidx0, out_fn, rstd, extra_fn=None, tiles=None):
        t0, t1 = (tiles or TILES)[ti]
        T = t1 - t0
        X = self.X
        for c in range(8):
            out, wres, view = out_fn(c)
            xin = X[:, c, t0:t1]
            rs = rstd[:, :T]
            if view is not None:
                xin, rs = view(xin), view(rs)
            self.stt(out, xin, self.cv(gname, gidx0 + c), rs, ALU.mult, ALU.mult, [("x", c, ti), "rstd", "cv"], [wres])
            if extra_fn is not None:
                extra_fn(c, rstd)

    def norm(self, ti, gname, gidx0, out_fn, sq, sd, rstd, extra_fn=None, tiles=None):
        self.norm_a(ti, sq, sd, rstd, tiles)
        self.norm_b(ti, gname, gidx0, out_fn, rstd, extra_fn, tiles)

    def norm_scratch(self):
        sq = self.scr(1024, bf=True).rearrange("p (c t) -> p c t", c=4)
        sd = self.scr(512)
        rstd = self.scr(512)
        return sq, sd, rstd

    def load_wblock(self, dst, src2d, col0, ncol, res, rows=D, key=None):
        if key is not None:
            if key in self.pref:
                return
            self.pref.add(key)
        src = src2d.rearrange("(k p) f -> p k f", p=128)[:, :, col0:col0 + ncol]
        self.dma("pool", dst, src, [], [res])

    def mixer_win(self, name, j):
        dst = self.wunit(j).rearrange("p (k f) -> p k f", k=8)
        self.load_wblock(dst, self.dr[name], j * 512, 512, ("W", j), key=(name, j))
        return dst

    def ffn_load(self, L, bi, jbase):
        blocks = [(0, 4), (4, 4), (8, 4), (12, 4), (16, 4), (20, 2)]
        f0, nf = blocks[bi]
        p = (jbase + bi) % 2
        w_in = self.dr["ffn_w_in"][L]
        w_out = self.dr["ffn_w_out"][L]
        wg = self.wunit(3 * p)[:, :8 * nf * 128].rearrange("p (k f) -> p k f", k=8)
        wu = self.wunit(3 * p + 1)[:, :8 * nf * 128].rearrange("p (k f) -> p k f", k=8)
        wo = self.wunit(3 * p + 2)[:, :nf * 1024].rearrange("p (j d) -> p j d", j=nf)
        if ("ffn", L, bi) not in self.pref:
            self.pref.add(("ffn", L, bi))
            self.load_wblock(wg, w_in, f0 * 128, nf * 128, ("W", 3 * p))
            self.load_wblock(wu, w_in, DFF + f0 * 128, nf * 128, ("W", 3 * p + 1))
            src = w_out[f0 * 128:(f0 + nf) * 128, :].rearrange("(j p) d -> p j d", p=128)
            self.dma("pool", wo, src, [], [("W", 3 * p + 2)])
        return (wg, wu, wo, p)

    def phase_input(self):
        self.scr_reset()
        stg = [self.scr(1024) for _ in range(4)]
        X = self.X
        n = 0
        import os
        for j in range(int(_dev("KNIN", "17"))):
            if j < 16:
                rows, R, col0 = self.dr["xp"][j * 128:(j + 1) * 128, :], 128, j * 128
            else:
                rows, R, col0 = self.dr["xs"], 64, SEQ
            ti = min(col0 // 512, 4)
            s = stg[j % 4]
            sres = ("stg", j % 4)

            def dst(c, ps, psres, col0=col0, R=R, ti=ti, j=j):
                nonlocal n
                n += 1
                eng = "act" if (c // 4) % 2 == 0 else "dve"
                self.cp(eng, X[:, c, col0:col0 + R], ps, [psres], [("x", c, ti)])
            self.load_T(rows, R, dst, None, s, sres)

    def phase_ffn(self, L, jbase, next_prefetch=None, pre_hook=None):
        self.scr_reset()
        sq, sd, rstd = self.norm_scratch()
        actb = [self.scr(1024, bf=True).rearrange("p (f t) -> p f t", f=4) for _ in range(2)]
        sg = [self.scr(512) for _ in range(2)]
        H = self.hreg(0, 8 * NTOK // 2, bf=True).rearrange("p (c t) -> p c t", c=8)
        X, PS = self.X, self.PS
        def do_norm(ti):
            if ti >= 5:
                return
            t0, t1 = FTILES[ti]
            self.norm(ti, "norm_ffn", L * 8, lambda c, t0=t0, t1=t1, ti=ti: (H[:, c, t0:t1], ("H", c, ti), None), sq, sd, rstd, tiles=FTILES)
        blocks = [(0, 4), (4, 4), (8, 4), (12, 4), (16, 4), (20, 2)]
        w_in = self.dr["ffn_w_in"][L]
        w_out = self.dr["ffn_w_out"][L]
        seq = []
        for bi, (f0, nf) in enumerate(blocks):
            for ti in range(5):
                seq.append((bi, ti))
        loaded = set()
        wv = {}

        def ensure_loaded(bi):
            if bi in loaded or bi >= len(blocks):
                return
            loaded.add(bi)
            wv[bi] = self.ffn_load(L, bi, jbase)

        def GU(n):
            bi, ti = seq[n]
            f0, nf = blocks[bi]
            wg, wu, wo, p = wv[bi]
            t0, t1 = FTILES[ti]
            T = t1 - t0
            ab = actb[n % 2]
            for fc in range(nf):
                bg, bu = self.bank(), self.bank()
                for k in range(8):
                    self.mm(PS[:, bg, :T], wg[:, k, fc * 128:(fc + 1) * 128], H[:, k, t0:t1], k == 0, k == 7,
                            [("W", 3 * p), ("H", k, ti)], [("ps", bg)])
                for k in range(8):
                    self.mm(PS[:, bu, :T], wu[:, k, fc * 128:(fc + 1) * 128], H[:, k, t0:t1], k == 0, k == 7,
                            [("W", 3 * p + 1), ("H", k, ti)], [("ps", bu)])
                s = sg[fc % 2]
                self.act(s[:, :T], PS[:, bg, :T], AF.Silu, [("ps", bg)], [("sg", fc % 2)])
                self.tt(ab[:, fc, :T], s[:, :T], PS[:, bu, :T], ALU.mult, [("sg", fc % 2), ("ps", bu)], [("actb", n % 2, fc)])

        def Y(n):
            bi, ti = seq[n]
            f0, nf = blocks[bi]
            wg, wu, wo, p = wv[bi]
            t0, t1 = FTILES[ti]
            T = t1 - t0
            ab = actb[n % 2]
            for m in range(8):
                b = self.bank()
                for fc in range(nf):
                    self.mm(PS[:, b, :T], wo[:, fc, m * 128:(m + 1) * 128], ab[:, fc, :T], fc == 0, fc == nf - 1,
                            [("W", 3 * p + 2), ("actb", n % 2, fc)], [("ps", b)])
                self.tt(X[:, m, t0:t1], X[:, m, t0:t1], PS[:, b, :T], ALU.add, [("x", m, ti), ("ps", b)], [("x", m, ti)])

        ensure_loaded(0)
        if pre_hook is not None:
            pre_hook()
        do_norm(0)
        for n in range(len(seq)):
            if seq[n][0] == 0:
                do_norm(seq[n][1] + 1)
            GU(n)
            if n > 0:
                Y(n - 1)
            if seq[n][1] == 0:
                ensure_loaded(seq[n][0] + 1)
                if seq[n][0] == len(blocks) - 1 and next_prefetch is not None:
                    next_prefetch()
        Y(len(seq) - 1)
        return jbase + len(blocks)

    def phase_final(self):
        self.scr_reset()
        sq, sd, rstd = self.norm_scratch()
        yts = [self.scr(4096).rearrange("p (c t) -> p c t", c=8) for _ in range(2)]
        stg = [self.scr(1024) for _ in range(3)]
        n = 0

        def fnorm(ti):
            if ti < 5:
                T = TILES[ti][1] - TILES[ti][0]
                yt = yts[ti % 2]
                self.norm(ti, "norm_final", 0, lambda c, T=T, yt=yt, ti=ti: (yt[:, c, :T], ("yt", ti % 2, c), None), sq, sd, rstd)
        fnorm(0)
        for ti in range(5):
            t0, t1 = TILES[ti]
            T = t1 - t0
            yt = yts[ti % 2]
            fnorm(ti + 1)
            for q in range((T + 127) // 128):
                R = min(128, T - q * 128)
                s, sres = stg[n % 3], ("ostg", n % 3)
                n += 1
                if ti < 4:
                    rows = self.dr["yp"][t0 + q * 128:t0 + q * 128 + R, :]
                else:
                    rows = self.dr["ys"]
                self.store_T(lambda c, q=q, R=R, yt=yt: yt[:, c, q * 128:q * 128 + R], [("yt", ti % 2, c) for c in range(8)], R,
                             lambda s_, r_, rows=rows, R=R: self.dma("sp", rows, s_[:R, :], [r_], []), s, sres)

    def phase_pool(self):
        self.scr_reset()
        sq, sd, rstd = self.norm_scratch()
        wsA = self.scr(528)
        wsB = self.scr(528)
        dt_ = self.scr(2048, bf=True).rearrange("p (c t) -> p c t", c=8)
        HF = self.scr(640).rearrange("p (c t) -> p c t", c=8)
        stg = self.scr(1024)
        tmp15 = [self.scr(16), self.scr(16)]
        identb = self.scr(64, bf=True)
        Iw = self.scr(512, bf=True).rearrange("p (j m) -> p j m", j=8)
        wic = self.scr(64).rearrange("p (g t) -> p g t", g=4)
        X, PS, dr = self.X, self.PS, self.dr
        self.cp("act", identb[:, :], self.ident[:, :], ["ident"], ["identb"])
        for g in range(4):
            w_ = 2.0 ** (g + 1)
            self.ts(Iw[:, 2 * g, :], identb[:, :], 1.0 / w_ - 1.0, None, ALU.mult, None, ["identb"], ["Iw"])
            self.ts(Iw[:, 2 * g + 1, :], identb[:, :], 1.0 / w_, None, ALU.mult, None, ["identb"], ["Iw"])
            self.ts(wic[:, g, :], self.invc[:, g, :], w_, None, ALU.mult, None, ["invc"], ["wic"])
        HPW = 2368
        HP = self.hreg(0, 8 * HPW // 2, bf=True).rearrange("p (c t) -> p c t", c=8)
        SB = 2064
        pw = self.wunit(5)[:, :2048].rearrange("p (g k e) -> p g k e", g=4, k=2)
        self.dma("pool", pw, dr["pool_w"].rearrange("g (k p) e -> p g k e", p=128), [], [("W", 5)])
        self.ffn_load(0, 0, 0)
        self.memset("pool", HP[:, :, 0:16], 0.0, ["hp_pad"])
        for half in range(2):
            def dst(c, ps, psres, half=half):
                o = HP[:, c, SB + half * 152:SB + half * 152 + 152].rearrange("p (b r) -> p b r", r=19)[:, :, 0:15]
                self.cp("act", o, ps.rearrange("p (b r) -> p b r", r=15), [psres], [("hp_s", c)])
            self.load_T(dr["spool"][half * 120:(half + 1) * 120, :], 120, dst, None, stg, "stg")
        self.dma("sp", dr["pool_s"].rearrange("(b r) d -> b r d", r=15)[:, 0:11, :],
                 dr["spool"].rearrange("(b r) d -> b r d", r=15)[:, 4:15, :], [], [])
        sview = lambda ap: ap.rearrange("p (b t) -> p b t", t=4)
        norm_jobs = []
        for ti in range(5):
            t0, t1 = TILES[ti]
            T = t1 - t0
            if ti < 4:
                ofn = lambda c, t0=t0, t1=t1, ti=ti: (HP[:, c, 16 + t0:16 + t1], ("hp", c, ti), None)
            else:
                ofn = lambda c: (HP[:, c, SB:SB + 304].rearrange("p (b r) -> p b r", r=19)[:, :, 15:19], ("hp_s", c), sview)
            extra = None
            if ti == 3:
                def extra(c, rstd_):
                    self.stt(HF[:, c, 0:15], X[:, c, 2033:2048], self.cv("norm_mix", c), rstd_[:, 497:512], ALU.mult, ALU.mult,
                             [("x", c, 3), "rstd", "cv"], [("hf", c)])
            if ti == 4:
                def extra(c, rstd_):
                    self.stt(HF[:, c, 16:80], X[:, c, 2048:2112], self.cv("norm_mix", c), rstd_[:, 0:64], ALU.mult, ALU.mult,
                             [("x", c, 4), "rstd", "cv"], [("hfs", c)])
            norm_jobs.append((ti, ofn, extra))
        def do_pool_norm(ti):
            if ti < 5:
                ti_, ofn, extra = norm_jobs[ti]
                self.norm(ti_, "norm_mix", 0, ofn, sq, sd, rstd, extra)
        do_pool_norm(0)
        for ti in range(5):
            t0, t1 = TILES[ti]
            T = t1 - t0
            do_pool_norm(ti + 1)
            for c in range(8):
                g = c // 2
                win = 2 ** (g + 1)
                b = self.bank()
                if ti < 4:
                    hres = [("hp", c, ti)] + ([("hp", c, ti - 1)] if ti > 0 else ["hp_pad"])
                else:
                    hres = [("hp_s", c)]
                for k in range(win):
                    lw = Iw[:, 2 * g + (0 if k == 0 else 1), :]
                    if ti < 4:
                        rhs = HP[:, c, 16 + t0 - k:16 + t0 - k + T]
                        po = PS[:, b, :T]
                    else:
                        rhs = HP[:, c, SB:SB + 304].rearrange("p (b r) -> p b r", r=19)[:, :, 15 - k:19 - k]
                        po = PS[:, b, :T].rearrange("p (b t) -> p b t", t=4)
                    self.mm(po, lw, rhs, k == 0, k == win - 1, ["Iw"] + hres, [("ps", b)])
                self.cp("act", dt_[:, c, :T], PS[:, b, :T], [("ps", b)], [("dt", c)])
                if ti == 0 and win > 1:
                    tmp = tmp15[c % 2]
                    hs = HP[:, c, 16:31]
                    self.cp("act", tmp[:, 0:15], PS[:, b, 0:15], [("ps", b)], [("wsx", c % 2)])
                    self.tt(tmp[:, 0:15], tmp[:, 0:15], hs, ALU.add, [("wsx", c % 2)] + hres, [("wsx", c % 2)])
                    self.tt(tmp[:, 0:15], tmp[:, 0:15], wic[:, g, 0:15], ALU.mult, [("wsx", c % 2), "wic"], [("wsx", c % 2)])
                    self.tt(dt_[:, c, 0:15], tmp[:, 0:15], hs, ALU.subtract, [("wsx", c % 2)] + hres, [("dt", c)])
            for g in range(4):
                for m in range(2):
                    b = self.bank()
                    for k in range(2):
                        self.mm(PS[:, b, :T], pw[:, g, k, m * 128:(m + 1) * 128], dt_[:, 2 * g + k, :T], k == 0, k == 1,
                                [("W", 5), ("dt", 2 * g + k)], [("ps", b)])
                    c = 2 * g + m
                    self.stt(X[:, c, t0:t1], PS[:, b, :T], self.cv("pool_scale", c), X[:, c, t0:t1], ALU.mult, ALU.add,
                             [("ps", b), ("x", c, ti), "cv"], [("x", c, ti)])
        self.store_T(lambda c: HF[:, c, 0:15], [("hf", c) for c in range(8)], 15,
                     lambda s_, r_: self.dma("sp", dr["pool_p"], s_[:15, :], [r_], []), stg, "stg")

        def sample_rows(s_, r_):
            for b in range(NSEQ):
                self.dma("sp", dr["pool_s"][b * 15 + 11:b * 15 + 15, :], s_[4 * b:4 * b + 4, :], [r_], [])
        self.store_T(lambda c: HF[:, c, 16:80], [("hfs", c) for c in range(8)], 64, sample_rows, stg, "stg")

    def phase_conv(self):
        self.scr_reset()
        sq, sd, rstd = self.norm_scratch()
        gexts = [self.scr(4 * 544, bf=True).rearrange("p (c t) -> p c t", c=8) for _ in range(2)]
        ct = self.scr(4096).rearrange("p (c t) -> p c t", c=8)
        stg = self.scr(1024)
        Dg = [self.scr(1984, bf=True).rearrange("p (k m) -> p k m", k=31),
              self.hreg(4096, 1984, bf=True).rearrange("p (k m) -> p k m", k=31)]
        identb = self.scr(64, bf=True)
        ht = self.hreg(0, 2048, bf=True).rearrange("p (c t) -> p c t", c=8)
        cn = self.hreg(2048, 2048, bf=True).rearrange("p (c t) -> p c t", c=8)
        cb = [self.hreg(6080 + i * 256, 256, bf=True) for i in range(2)]
        cq = [self.hreg(6592 + i * 256, 256, bf=True) for i in range(2)]
        sig = [self.hreg(7104, 512)]
        mean_sb = self.hreg(7616, 512)
        mv = self.hreg(8128, 512)
        gsm = self.hreg(8640, 512).rearrange("p (c t) -> p c t", c=8)
        gfp = self.hreg(9152, 240).rearrange("p (c t) -> p c t", c=8)
        X, PS, dr = self.X, self.PS, self.dr
        win = [self.mixer_win("conv_w_in", j) for j in range(4)]
        wout = [self.wunit(4 + j).rearrange("p (k f) -> p k f", k=8) for j in range(2)]
        for j in range(2):
            self.load_wblock(wout[j], dr["conv_w_out"], j * 512, 512, ("W", 4 + j))
        self.cp("act", identb[:, :], self.ident[:, :], ["ident"], ["identb"])
        self.dma("sp", dr["conv_s"].rearrange("(b r) d -> b r d", r=30)[:, 0:26, :],
                 dr["sconv"].rearrange("(b r) d -> b r d", r=30)[:, 4:30, :], [], [])
        sview = lambda ap: ap.rearrange("p (b t) -> p b t", t=4)
        dwc = self.cvcol["conv_dw"]
        self.ndg = 0

        def tinfo(ti):
            t0, t1 = TILES[ti]
            return t0, t1, t1 - t0, ti == 4

        def stage_N(ti):
            t0, t1, T, samp = tinfo(ti)
            self.norm(ti, "norm_mix", 8, lambda c, T=T: (ht[:, c, :T], ("ht", c), None), sq, sd, rstd)

        def stage_A(ti):
            t0, t1, T, samp = tinfo(ti)
            gext = gexts[ti % 2]
            gres = ("gext", ti % 2)
            prev, pres = gexts[(ti - 1) % 2], ("gext", (ti - 1) % 2)
            if ti == 0:
                self.memset("pool", gext[:, :, 0:30], 0.0, [gres])
            elif not samp:
                self.cp("pool", gext[:, :, 0:30], prev[:, :, 512:542], [pres], [gres])
            else:
                for q in range(4):
                    def dst(c, ps, psres, q=q):
                        o = gext[:, c, q * 136:(q + 1) * 136].rearrange("p (b r) -> p b r", r=34)[:, :, 0:30]
                        self.cp("act", o, ps.rearrange("p (b r) -> p b r", r=30), [psres], [gres])
                    self.load_T(dr["sconv"][q * 120:(q + 1) * 120, :], 120, dst, None, stg, "stg")
            for m in range(8):
                ba, bg = self.bank(), self.bank()
                for k in range(8):
                    self.mm(PS[:, ba, :T], win[m // 4][:, k, (m % 4) * 128:(m % 4 + 1) * 128], ht[:, k, :T], k == 0, k == 7,
                            [("W", m // 4), ("ht", k)], [("ps", ba)])
                for k in range(8):
                    self.mm(PS[:, bg, :T], win[2 + m // 4][:, k, (m % 4) * 128:(m % 4 + 1) * 128], ht[:, k, :T], k == 0, k == 7,
                            [("W", 2 + m // 4), ("ht", k)], [("ps", bg)])
                s = sig[0]
                self.act(s[:, :T], PS[:, bg, :T], AF.Sigmoid, [("ps", bg), "cv"], ["sig"], bias=self.cv("conv_b_in", 8 + m))
                if not samp:
                    o, pa, sv = gext[:, m, 30:30 + T], PS[:, ba, :T], s[:, :T]
                else:
                    o = gext[:, m, 0:544].rearrange("p (b r) -> p b r", r=34)[:, :, 30:34]
                    pa, sv = sview(PS[:, ba, :T]), sview(s[:, :T])
                self.stt(o, pa, self.cv("conv_b_in", m), sv, ALU.add, ALU.mult, [("ps", ba), "sig", "cv", gres], [gres, ("gx", m)])
                if ti == 3:
                    self.stt(gfp[:, m, :], PS[:, ba, 482:512], self.cv("conv_b_in", m), s[:, 482:512], ALU.add, ALU.mult,
                             [("ps", ba), "sig", "cv"], [("gfp", m)])
                if samp:
                    self.stt(gsm[:, m, :], PS[:, ba, :T], self.cv("conv_b_in", m), s[:, :T], ALU.add, ALU.mult,
                             [("ps", ba), "sig", "cv"], [("gsm", m)])
            if ti == 3:
                self.store_T(lambda c: gfp[:, c, :], [("gfp", c) for c in range(8)], 30,
                             lambda s_, r_: self.dma("sp", dr["conv_p"], s_[:30, :], [r_], []), stg, "stg")
            if samp:
                def sample_rows(s_, r_):
                    for b in range(NSEQ):
                        self.dma("sp", dr["conv_s"][b * 30 + 26:b * 30 + 30, :], s_[4 * b:4 * b + 4, :], [r_], [])
                self.store_T(lambda c: gsm[:, c, :], [("gsm", c) for c in range(8)], 64, sample_rows, stg, "stg")
                self.ffn_load(self.cur_L, 0, self.cur_j)

        def build_dg(c):
            wtap = self.CV[:, dwc + c:dwc + c + 31 * 8:8].unsqueeze(2).to_broadcast([128, 31, 128])
            idb = identb.unsqueeze(1).to_broadcast([128, 31, 128])
            self.tt(Dg[c % 2][:, :, :], idb, wtap, ALU.mult, ["identb", "cv"], [("dg", c % 2)])

        def stage_C(ti):
            t0, t1, T, samp = tinfo(ti)
            gext = gexts[ti % 2]
            gres = ("gext", ti % 2)
            b1, b2 = self.bank(), self.bank()
            self.lnb12 = (b1, b2)
            self.reserved |= {b1, b2}

            def stats_mm(c):
                self.mm(PS[:, b1, :T], self.ones[:, :], cb[c % 2][:, :T], c == 0, c == 7, [("cb", c % 2), "ones"], [("ps", b1)])
                self.mm(PS[:, b2, :T], self.ones[:, :], cq[c % 2][:, :T], c == 0, c == 7, [("cq", c % 2), "ones"], [("ps", b2)])
            for c in range(8):
                dg = Dg[c % 2]
                dres = ("dg", c % 2)
                if c >= 2:
                    build_dg(c)
                b = self.bank()
                for k in range(31):
                    if not samp:
                        gk = gext[:, c, k:k + T]
                        po = PS[:, b, :T]
                    else:
                        gk = gext[:, c, 0:544].rearrange("p (b r) -> p b r", r=34)[:, :, k:k + 4]
                        po = sview(PS[:, b, :T])
                    self.mm(po, dg[:, k, :], gk, k == 0, k == 30, [dres, gres], [("ps", b)])
                if c >= 1:
                    stats_mm(c - 1)
                self.act(ct[:, c, :T], PS[:, b, :T], AF.Identity, [("ps", b), "cv"], [("ct", c)], bias=self.cv("conv_dw_b", c))
                self.cp("act", cb[c % 2][:, :T], ct[:, c, :T], [("ct", c)], [("cb", c % 2)])
                self.act(cq[c % 2][:, :T], ct[:, c, :T], AF.Square, [("ct", c)], [("cq", c % 2)])
            stats_mm(7)

        def stage_L1(ti):
            t0, t1, T, samp = tinfo(ti)
            b1, b2 = self.lnb12
            self.cp("act", mean_sb[:, :T], PS[:, b1, :T], [("ps", b1)], ["mean"])
            self.act(mv[:, :T], PS[:, b1, :T], AF.Square, [("ps", b1)], ["mv"])
            self.tt(mv[:, :T], PS[:, b2, :T], mv[:, :T], ALU.subtract, [("ps", b2), "mv"], ["mv"])
            self.reserved -= {b1, b2}
            self.act(mv[:, :T], mv[:, :T], AF.Sqrt, ["mv", "eps"], ["mv"], bias=self.eps[:, 0:1])
            self.recip(mv[:, :T], mv[:, :T], ["mv"], ["mv"])
            for c in range(8):
                self.tt(ct[:, c, :T], ct[:, c, :T], mean_sb[:, :T], ALU.subtract, [("ct", c), "mean"], [("ct", c)])
            for c in range(8):
                self.tt(ct[:, c, :T], ct[:, c, :T], mv[:, :T], ALU.mult, [("ct", c), "mv"], [("ct", c)])

        def stage_L2(ti):
            t0, t1, T, samp = tinfo(ti)
            for c in range(8):
                self.act(cn[:, c, :T], ct[:, c, :T], AF.Silu, [("ct", c), "cv"], [("cn", c)],
                         scale=self.cv("conv_ln_g", c), bias=self.cv("conv_ln_b", c))

        def stage_O(ti):
            t0, t1, T, samp = tinfo(ti)
            for m in range(8):
                b = self.bank()
                for k in range(8):
                    self.mm(PS[:, b, :T], wout[m // 4][:, k, (m % 4) * 128:(m % 4 + 1) * 128], cn[:, k, :T], k == 0, k == 7,
                            [("W", 4 + m // 4), ("cn", k)], [("ps", b)])
                self.tt(X[:, m, t0:t1], X[:, m, t0:t1], PS[:, b, :T], ALU.add, [("x", m, ti), ("ps", b)], [("x", m, ti)])

        stage_N(0)
        build_dg(0)
        build_dg(1)
        stage_A(0)
        for ti in range(5):
            nxt = ti + 1 < 5
            stage_C(ti)
            if nxt:
                stage_N(ti + 1)
            stage_L1(ti)
            if nxt:
                stage_A(ti + 1)
            stage_L2(ti)
            if nxt:
                build_dg(0)
                build_dg(1)
            stage_O(ti)

    def phase_gmlp(self, part="all"):
        save_off = self.scr_off
        self.scr_reset()
        sq, sd, rstd = self.norm_scratch()
        ut = self.scr(2048, bf=True).rearrange("p (c t) -> p c t", c=8)
        utsp = self.scr(512)
        vg = [self.scr(1024) for _ in range(4)]
        lng = self.scr(1024)
        lnb = self.scr(1024)
        Cb = self.scr(1024).rearrange("p (c t) -> p c t", c=8)
        tmp = [self.scr(512), self.scr(512)]
        stats = self.scr(72)
        ht = self.hreg(0, 2048, bf=True).rearrange("p (c t) -> p c t", c=8)
        yb = self.hreg(2048, 2048, bf=True).rearrange("p (c t) -> p c t", c=8)
        vb = [self.hreg(4096 + i * 2048, 2048, bf=True).rearrange("p (q e) -> p q e", q=4) for i in range(2)]
        wsTb = self.hreg(8448, 256, bf=True).rearrange("p (g t) -> p g t", g=4)
        BDb = self.hreg(8704, 128, bf=True).rearrange("p (g t) -> p g t", g=4)
        CS = self.hreg(8832, 512).rearrange("p (c t) -> p c t", c=8)
        so = O_SCR + 5120
        wstg = self.f32(so, 512).rearrange("p (g s) -> p g s", g=4)
        wsTf = self.f32(so + 512, 512).rearrange("p (g t) -> p g t", g=4)
        tri = self.f32(so + 1024, 128)
        BDr = self.f32(so + 1152, 256).rearrange("p (g t) -> p g t", g=4)
        bsb = self.f32(so + 1408, 512).rearrange("p (g t) -> p g t", g=4)
        bsbS = self.f32(so + 1920, 256).rearrange("p (g t) -> p g t", g=4)
        onesf = self.f32(so + 2176, 128)
        X, PS, dr, S = self.X, self.PS, self.dr, self.S
        def gm_setup():
            self.dma("sp", lng[:, :], dr["gm_ln_g"].to_broadcast([128, D]), [], ["lng"])
            self.dma("sp", lnb[:, :], dr["gm_ln_b"].to_broadcast([128, D]), [], ["lnb"])
            for g in range(4):
                self.dma("sp", bsb[:, g, :], dr["gm_b_s"][g:g + 1, :].to_broadcast([128, 128]), [], [("bsb", g)])
            self.dma("sp", wstg, dr["gm_w_s"].rearrange("g t s -> t g s"), [], ["wstg"])
            self.memset("pool", tri[:, :], 1.0, ["tri"])
            S.op("pool", lambda e: e.affine_select(out=tri[:, :], in_=tri[:, :], pattern=[[1, 128]], compare_op=ALU.is_ge,
                                                   fill=0.0, base=0, channel_multiplier=-1), ["tri"], ["tri"])
            self.memset("pool", onesf[:, :], 1.0, ["onesf"])
            b = self.bank()
            for g in range(4):
                self.tr(PS[:, b, g * 128:(g + 1) * 128], wstg[:, g, :], self.ident[:, :], ["wstg", "ident"], [("ps", b)])
            for g in range(4):
                self.tt(wsTf[:, g, :], PS[:, b, g * 128:(g + 1) * 128], tri[:, :], ALU.mult, [("ps", b), "tri"], ["wsTf"])
            self.cp("act", wsTb[:, :, :], wsTf[:, :, :], ["wsTf"], ["wsTb"])
            self.memset("pool", BDr[:, :, :], 0.0, ["BDr"])
            for bq in range(NSEQ):
                self.dma("sp", BDr[4 * bq:4 * bq + 4, :, 4 * bq:4 * bq + 4], wsTf[0:4, :, 0:4], ["wsTf", "BDr"], [("BDr", bq)])
            bdres = [("BDr", bq) for bq in range(NSEQ)]
            self.cp("act", BDb[:64, :, :], BDr[:64, :, :], bdres, ["BDb"])
            for g in range(4):
                self.cp("pool", bsbS[:, g, :].rearrange("p (b t) -> p b t", t=4), bsb[:, g, 0:4].unsqueeze(1).to_broadcast([128, 16, 4]),
                        [("bsb", g)], ["bsbS"])
            b2 = self.bank()
            for g in range(4):
                self.mm(PS[:, b2, g * 128:(g + 1) * 128], onesf[:, :], wsTf[:, g, :], True, True, ["onesf", "wsTf"], [("ps", b2)])
            for cc in range(8):
                g = cc // 2
                self.stt(Cb[:, cc, :], PS[:, b2, g * 128:(g + 1) * 128], self.cv("gm_ln_b", cc), bsb[:, g, :], ALU.mult, ALU.add,
                         [("ps", b2), "cv", ("bsb", g)], ["Cb"])
            b3 = self.bank()
            for g in range(4):
                self.mm(PS[:, b3, g * 64:(g + 1) * 64], onesf[:64, :], BDr[:64, g, :], True, True, ["onesf"] + bdres, [("ps", b3)])
            for cc in range(8):
                g = cc // 2
                self.stt(CS[:, cc, :], PS[:, b3, g * 64:(g + 1) * 64], self.cv("gm_ln_b", cc), bsbS[:, g, :], ALU.mult, ALU.add,
                         [("ps", b3), "cv", "bsbS"], ["CS"])

        if part in ("all", "setup") and not getattr(self, "gm_setup_done", False):
            self.gm_setup_done = True
            gm_setup()
        if part == "setup":
            self.scr_off = save_off
            return
        win = [self.mixer_win("gm_w_in", j) for j in range(4)]
        wout = [self.wunit(4 + j).rearrange("p (k f) -> p k f", k=8) for j in range(2)]
        for j in range(2):
            self.load_wblock(wout[j], dr["gm_w_out"], j * 512, 512, ("W", 4 + j))
        if part == "all":
            S.barrier()

        def tinfo(ti):
            t0, t1 = TILES[ti]
            return t0, t1, t1 - t0, ti == 4

        def stage_norm(ti):
            stage_norm_a(ti)
            stage_norm_b(ti)

        def stage_norm_a(ti):
            self.norm_a(ti, sq, sd, rstd)

        def stage_norm_b(ti):
            t0, t1, T, samp = tinfo(ti)
            self.norm_b(ti, "norm_mix", 16, lambda c, T=T: (ht[:, c, :T], ("ht", c), None), rstd)

        def stage_U(ti):
            t0, t1, T, samp = tinfo(ti)
            for m in range(8):
                b = self.bank()
                for k in range(8):
                    self.mm(PS[:, b, :T], win[m // 4][:, k, (m % 4) * 128:(m % 4 + 1) * 128], ht[:, k, :T], k == 0, k == 7,
                            [("W", m // 4), ("ht", k)], [("ps", b)])
                self.act(ut[:, m, :T], PS[:, b, :T], AF.Gelu_apprx_tanh, [("ps", b)], [("ut", m)])

        def stage_V(ti):
            t0, t1, T, samp = tinfo(ti)
            nq = 1 if samp else 4
            R = 64 if samp else 128
            for q in range(nq):
                v = vg[q]
                for eh in range(2):
                    b = self.bank()
                    for k in range(8):
                        self.mm(PS[:R, b, :], ht[:, k, q * 128:q * 128 + R], win[2 + eh][:, k, :], k == 0, k == 7,
                                [("W", 2 + eh), ("ht", k)], [("ps", b)])
                    self.act(v[:R, eh * 512:(eh + 1) * 512], PS[:R, b, :], AF.Gelu_apprx_tanh, [("ps", b)], [("vg", q)])

        def stage_LN(ti):
            t0, t1, T, samp = tinfo(ti)
            nq = 1 if samp else 4
            R = 64 if samp else 128
            vbt = vb[ti % 2]
            st6 = stats[:, 0:48].rearrange("p (q a s) -> p q a s", q=4, a=2)
            mv = stats[:, 48:56].rearrange("p (q s) -> p q s", q=4)
            for q in range(nq):
                for eh in range(2):
                    S.op("dve", (lambda q=q, eh=eh, R=R: (lambda e: e.bn_stats(out=st6[:R, q, eh, :], in_=vg[q][:R, eh * 512:(eh + 1) * 512])))(),
                         [("vg", q)], [("s6", q, eh)])
                S.op("dve", (lambda q=q, R=R: (lambda e: e.bn_aggr(out=mv[:R, q, :], in_=st6[:R, q, :, :].rearrange("p a s -> p (a s)"))))(),
                     [("s6", q, 0), ("s6", q, 1)], ["mv"])
            sd4, rs4, nb4 = stats[:, 56:60], stats[:, 60:64], stats[:, 64:68]
            self.act(sd4[:R, :nq], mv[:R, :nq, 1], AF.Sqrt, ["mv", "eps"], ["sd4"], bias=self.eps[:R, 0:1])
            self.recip(rs4[:R, :nq], sd4[:R, :nq], ["sd4"], ["rs4"])
            self.stt(nb4[:R, :nq], mv[:R, :nq, 0], -1.0, rs4[:R, :nq], ALU.mult, ALU.mult, ["mv", "rs4"], ["nb4"])
            for q in range(nq):
                v = vg[q]
                self.act(vbt[:R, q, :], v[:R, :], AF.Identity, [("vg", q), "rs4", "nb4"], [("vb", ti % 2, q)],
                         scale=rs4[:R, q:q + 1], bias=nb4[:R, q:q + 1])
                if samp or (ti == 3 and q == 3):
                    self.stt(v[:R, :], v[:R, :], mv[:R, q, 0:1], lng[:R, :], ALU.subtract, ALU.mult, [("vg", q), "mv", "lng"], [("vg", q)])
                    self.stt(v[:R, :], v[:R, :], rs4[:R, q:q + 1], lnb[:R, :], ALU.mult, ALU.add, [("vg", q), "rs4", "lnb"], [("vg", q)])
                    if samp:
                        self.dma("sp", dr["v_s"], v[:64, :], [("vg", q)], [])
                    else:
                        self.dma("sp", dr["v_p"], v[:, :], [("vg", q)], [])

        def stage_S(ti):
            t0, t1, T, samp = tinfo(ti)
            vbt = vb[ti % 2]
            for cc in range(8):
                g = cc // 2
                b = self.bank()
                if not samp:
                    for q in range(4):
                        self.mm(PS[:, b, q * 128:(q + 1) * 128], vbt[:, q, cc * 128:(cc + 1) * 128], wsTb[:, g, :], True, True,
                                [("vb", ti % 2, q), "wsTb"], [("ps", b)])
                    self.stt(tmp[cc % 2][:, :T].rearrange("p (q t) -> p q t", q=4), PS[:, b, :T].rearrange("p (q t) -> p q t", q=4),
                             self.cv("gm_ln_g", cc), Cb[:, cc, :].unsqueeze(1).to_broadcast([128, 4, 128]), ALU.mult, ALU.add,
                             [("ps", b), "cv", "Cb"], [("tmp", cc % 2)])
                else:
                    self.mm(PS[:, b, :64], vbt[:64, 0, cc * 128:(cc + 1) * 128], BDb[:64, g, :], True, True, [("vb", ti % 2, 0), "BDb"], [("ps", b)])
                    self.stt(tmp[cc % 2][:, :T], PS[:, b, :T], self.cv("gm_ln_g", cc), CS[:, cc, :], ALU.mult, ALU.add,
                             [("ps", b), "cv", "CS"], [("tmp", cc % 2)])
                self.tt(yb[:, cc, :T], tmp[cc % 2][:, :T], ut[:, cc, :T], ALU.mult, [("tmp", cc % 2), ("ut", cc)], [("yb", cc)])

        def stage_O(ti):
            t0, t1, T, samp = tinfo(ti)
            for m in range(8):
                b = self.bank()
                for k in range(8):
                    self.mm(PS[:, b, :T], wout[m // 4][:, k, (m % 4) * 128:(m % 4 + 1) * 128], yb[:, k, :T], k == 0, k == 7,
                            [("W", 4 + m // 4), ("yb", k)], [("ps", b)])
                self.tt(X[:, m, t0:t1], X[:, m, t0:t1], PS[:, b, :T], ALU.add, [("x", m, ti), ("ps", b)], [("x", m, ti)])

        stage_norm(0)
        stage_V(0)
        stage_LN(0)
        stage_U(0)
        stage_norm(1)
        for ti in range(5):
            nxt = ti + 1 < 5
            stage_S(ti)
            if nxt:
                stage_V(ti + 1)
            if ti + 2 < 5:
                stage_norm_a(ti + 2)
            if nxt:
                stage_LN(ti + 1)
                stage_U(ti + 1)
            if ti + 2 < 5:
                stage_norm_b(ti + 2)
            if ti == 3:
                self.ffn_load(self.cur_L, 0, self.cur_j)
            stage_O(ti)

    def phase_sc(self):
        self.scr_reset()
        sq, sd, rstd = self.norm_scratch()
        cxe = self.scr(8 * 514).rearrange("p (c t) -> p c t", c=8)
        acc = [self.scr(512) for _ in range(2)]
        stg = self.scr(1024)
        csm = self.scr(256).rearrange("p (c t) -> p c t", c=8)
        ht = self.hreg(0, 2048, bf=True).rearrange("p (c t) -> p c t", c=8)
        ybs = [self.hreg(2048, 2048, bf=True).rearrange("p (c t) -> p c t", c=8),
               self.scr(2048, bf=True).rearrange("p (c t) -> p c t", c=8)]
        wout = [self.hreg(4096 + j * 2048, 2048, bf=True).rearrange("p (k f) -> p k f", k=8) for j in range(2)]
        cgs = [self.hreg(8192 + i * 512, 512) for i in range(2)]
        bbs = [self.scr(512) for _ in range(2)]
        X, PS, dr = self.X, self.PS, self.dr
        win = [self.mixer_win("sc_w_in", j) for j in range(6)]
        for j in range(2):
            self.load_wblock(wout[j], dr["sc_w_out"], j * 512, 512, ("scwo", j))
        sview = lambda ap: ap.rearrange("p (b t) -> p b t", t=4)

        def front(ti):
            t0, t1 = TILES[ti]
            T = t1 - t0
            samp = ti == 4
            yb = ybs[ti % 2]
            if ti == 0:
                self.memset("pool", cxe[:, :, 0:2], 0.0, ["cxe"])
            elif not samp:
                self.cp("pool", cxe[:, :, 0:2], cxe[:, :, 512:514], ["cxe"], ["cxe"])
            else:
                def dst(c, ps, psres):
                    o = cxe[:, c, 0:96].rearrange("p (b r) -> p b r", r=6)[:, :, 0:2]
                    self.cp("act", o, ps.rearrange("p (b r) -> p b r", r=2), [psres], ["cxe"])
                self.load_T(dr["ssc"], 32, dst, None, stg, "stg")
            for m in range(8):
                bb, bc, bx = self.bank(), self.bank(), self.bank()
                for (bk, j0) in ((bb, 0), (bc, 2), (bx, 4)):
                    for k in range(8):
                        self.mm(PS[:, bk, :T], win[j0 + m // 4][:, k, (m % 4) * 128:(m % 4 + 1) * 128], ht[:, k, :T], k == 0, k == 7,
                                [("W", j0 + m // 4), ("ht", k)], [("ps", bk)])
                cg = cgs[m % 2]
                self.cp("act", cg[:, :T], PS[:, bc, :T], [("ps", bc)], [("cgs", m % 2)])
                bsb_ = bbs[m % 2]
                self.cp("act", bsb_[:, :T], PS[:, bb, :T], [("ps", bb)], [("bbs", m % 2)])
                a = acc[m % 2]
                ares = ("acc", m % 2)
                if not samp:
                    self.tt(cxe[:, m, 2:2 + T], cg[:, :T], PS[:, bx, :T], ALU.mult, [("cgs", m % 2), ("ps", bx), "cxe"], ["cxe", ("cx", m)])
                    sl = lambda k, m=m, T=T: cxe[:, m, k:k + T]
                    av = a[:, :T]
                    pb = bsb_[:, :T]
                    yo = yb[:, m, :T]
                else:
                    ev = cxe[:, m, 0:96].rearrange("p (b r) -> p b r", r=6)
                    self.tt(ev[:, :, 2:6], sview(cg[:, :T]), sview(PS[:, bx, :T]), ALU.mult, [("cgs", m % 2), ("ps", bx), "cxe"], ["cxe", ("cx", m)])
                    sl = lambda k, ev=ev: ev[:, :, k:k + 4]
                    av = sview(a[:, :T])
                    pb = sview(bsb_[:, :T])
                    yo = sview(yb[:, m, :T])
                self.ts(av, sl(0), self.cv("sc_conv", m), None, ALU.mult, None, ["cxe", "cv"], [ares])
                self.stt(av, sl(1), self.cv("sc_conv", 8 + m), av, ALU.mult, ALU.add, ["cxe", "cv", ares], [ares])
                self.stt(av, sl(2), self.cv("sc_conv", 16 + m), av, ALU.mult, ALU.add, ["cxe", "cv", ares], [ares])
                self.tt(yo, av, pb, ALU.mult, [ares, ("bbs", m % 2)], [("yb", ti % 2, m)])
            if ti == 3:
                self.store_T(lambda c: cxe[:, c, 512:514], ["cxe"], 2,
                             lambda s_, r_: self.dma("sp", dr["sc_p"], s_[:2, :], [r_], []), stg, "stg")
            if samp:
                for c in range(8):
                    self.cp("act", csm[:, c, :].rearrange("p (b t) -> p b t", t=2),
                            cxe[:, c, 0:96].rearrange("p (b r) -> p b r", r=6)[:, :, 4:6], ["cxe"], [("csm", c)])
                self.store_T(lambda c: csm[:, c, :], [("csm", c) for c in range(8)], 32,
                             lambda s_, r_: self.dma("sp", dr["sc_s"], s_[:32, :], [r_], []), stg, "stg")
                self.ffn_load(self.cur_L, 0, self.cur_j)

        def back(ti):
            t0, t1 = TILES[ti]
            T = t1 - t0
            yb = ybs[ti % 2]
            for m in range(8):
                b = self.bank()
                for k in range(8):
                    self.mm(PS[:, b, :T], wout[m // 4][:, k, (m % 4) * 128:(m % 4 + 1) * 128], yb[:, k, :T], k == 0, k == 7,
                            [("scwo", m // 4), ("yb", ti % 2, k)], [("ps", b)])
                self.tt(X[:, m, t0:t1], X[:, m, t0:t1], PS[:, b, :T], ALU.add, [("x", m, ti), ("ps", b)], [("x", m, ti)])

        def snorm(ti):
            T = TILES[ti][1] - TILES[ti][0]
            self.norm(ti, "norm_mix", 24, lambda c, T=T: (ht[:, c, :T], ("ht", c), None), sq, sd, rstd)

        snorm(0)
        for ti in range(5):
            front(ti)
            if ti + 1 < 5:
                snorm(ti + 1)
            if ti >= 1:
                back(ti - 1)
        back(4)

    def run(self):
        import os
        S = self.S
        stop = _dev("KPH", "all")
        self.setup_consts()
        S.barrier()
        if stop != "consts":
            self.phase_input()
            S.barrier()
        j = 0
        names = ("pool", "conv", "gmlp", "sc")
        wnames = (None, "conv_w_in", "gm_w_in", "sc_w_in")
        skip = _dev("KSKIP", "").split(",")
        for L, ph in enumerate((self.phase_pool, self.phase_conv, self.phase_gmlp, self.phase_sc)):
            if stop in ("input", "consts"):
                break
            self.cur_L, self.cur_j = L, j
            if names[L] not in skip:
                if names[L] == "gmlp" and getattr(self, "gm_setup_done", False):
                    ph("main")
                else:
                    ph()
                S.barrier()
            if stop == names[L]:
                break
            if "ffn" not in skip:
                nxt = None
                if L + 1 < 4:
                    nxt = (lambda nm=wnames[L + 1]: [self.mixer_win(nm, jj) for jj in range(3)])
                pre = (lambda: self.phase_gmlp("setup")) if (L == 1 and "gmlp" not in skip and stop not in ("ffn1",)) else None
                j = self.phase_ffn(L, j, nxt, pre)
                S.barrier()
            if stop == "ffn%d" % L:
                break
        if _dev("KNOFINAL") is None:
            self.phase_final()


_NC = None


def build():
    nc = bass.Bass("TRN2", target_bir_lowering=False)
    dr = {}
    for n in IN_NAMES:
        dr[n] = nc.dram_tensor(n, IN_SHAPES[n], F32, kind="ExternalInput").ap()
    for n, s in OUT_SHAPES.items():
        dr[n] = nc.dram_tensor(n, s, F32, kind="ExternalOutput").ap()
    S = Sched()
    with ExitStack() as st:
        A = st.enter_context(nc.sbuf_tensor("arena", [128, NW], F32))
        PS = st.enter_context(nc.psum_tensor("ps", [128, 8, 512], F32))
        k = Kern(nc, S, A, PS, dr)
        k.run()
        S.emit(nc, st)
    return nc


def kernel(**inp):
    global _NC
    if _NC is None:
        _NC = build()
    nc = _NC
    f = lambda a: np.ascontiguousarray(np.asarray(a, dtype=np.float32))
    shared = {}
    for n in IN_NAMES[5:]:
        shared[n] = f(inp[n]).reshape(IN_SHAPES[n])
    xp, xs = f(inp["x_prompt"]), f(inp["x_sample"])
    sp_, sc_, ss_ = f(inp["state_pool"]), f(inp["state_conv"]), f(inp["state_shortconv"])
    in_maps = []
    for c in range(8):
        m = dict(shared)
        sl = slice(NSEQ * c, NSEQ * (c + 1))
        m["xp"] = xp[c]
        m["xs"] = np.ascontiguousarray(xs[sl].reshape(64, D))
        m["spool"] = np.ascontiguousarray(sp_[0, sl].reshape(240, D))
        m["sconv"] = np.ascontiguousarray(sc_[0, sl].reshape(480, D))
        m["ssc"] = np.ascontiguousarray(ss_[0, sl].reshape(32, D))
        in_maps.append(m)
    import os
    ncores = int(_dev("KCORES", "8"))
    res = run_bass_kernel_spmd(nc, in_maps[:ncores], core_ids=list(range(ncores)))
    R = list(res.results)
    while len(R) < 8:
        R.append(R[0])
    cat = lambda n, shp: np.stack([np.asarray(R[c][n], dtype=np.float32).reshape(shp) for c in range(8)])
    y_p = cat("yp", (SEQ, D))
    y_s = cat("ys", (NSEQ, DEC, D)).reshape(128, DEC, D)
    pool_p = cat("pool_p", (15, D))[None]
    pool_s = cat("pool_s", (NSEQ, 15, D)).reshape(128, 15, D)[None]
    conv_p = cat("conv_p", (30, D))[None]
    conv_s = cat("conv_s", (NSEQ, 30, D)).reshape(128, 30, D)[None]
    v_p = cat("v_p", (128, D))[None]
    v_s = cat("v_s", (NSEQ, DEC, D)).reshape(128, DEC, D)[None]
    sc_p = cat("sc_p", (2, D))[None]
    sc_s = cat("sc_s", (NSEQ, 2, D)).reshape(128, 2, D)[None]
    return (y_p, y_s, pool_p, pool_s, conv_p, conv_s, v_p, v_s, sc_p, sc_s)
```
